# Optimizing a Trainium2 kernel written in Bass

```python
import jax, jax.numpy as jnp
from jax import lax
import numpy as np

D_MODEL = 1024
BATCH = 2
SEQ = 8192
DEPTH = 2

CHUNK = 64
SB_BLOCK = 128
EPS = 1e-6
GLA_HEADS = 4
GLA_HEAD_K = 64
GLA_HEAD_V = 128
GLA_DK = GLA_HEADS * GLA_HEAD_K
GLA_DV = GLA_HEADS * GLA_HEAD_V
GLA_GATE_RANK = 16
GLA_GATE_NORMALIZER = 16.0
SB_HEADS = 8
SB_HEAD_DIM = 64
SB_D = SB_HEADS * SB_HEAD_DIM
MIX_WIDTH = GLA_DV + SB_D
IN_WIDTH = 2 * GLA_DK + 2 * GLA_DV + GLA_GATE_RANK + 3 * SB_D
CONV_WIDTH = 31
D_FF = -(-8 * D_MODEL // (3 * 256)) * 256
N_EVEN = (DEPTH + 1) // 2
N_ODD = DEPTH // 2

kernel_name = "hybrid_gla_stickbreaking_conformer_trunk"


def _rms_f32(xf, g):
    return xf * lax.rsqrt(jnp.mean(xf * xf, axis=-1, keepdims=True) + EPS) * g.astype(jnp.float32)


def rms_norm(x, g):
    return _rms_f32(x.astype(jnp.float32), g).astype(x.dtype)


def gla_mixer(q, k, v, r, gate_lr, w_gate2, b_gate, g_out):
    f32 = jnp.float32
    B, T, _ = q.shape
    nc = T // CHUNK
    log_a = jax.nn.log_sigmoid((gate_lr @ w_gate2 + b_gate).astype(f32)) / GLA_GATE_NORMALIZER

    def split(t, hd):
        return t.astype(f32).reshape(B, nc, CHUNK, -1, hd).transpose(0, 3, 1, 2, 4)

    qc = split(q, GLA_HEAD_K) * (GLA_HEAD_K ** -0.5)
    kc = split(k, GLA_HEAD_K)
    vc = split(v, GLA_HEAD_V)
    bc = jnp.cumsum(split(log_a, GLA_HEAD_K), axis=3)
    b_end = bc[:, :, :, -1:, :]
    k_end = kc * jnp.exp(b_end - bc)
    scores = jnp.einsum('bhnik,bhnjk->bhnij', qc, k_end)
    intra = jnp.einsum('bhnij,bhnjv->bhniv', scores, vc)
    chunk_update = jnp.einsum('bhnjk,bhnjv->bhnkv', k_end, vc)
    chunk_decay = jnp.exp(b_end[:, :, :, 0, :])

    def step(s, inp):
        a, u = inp
        return a[..., None] * s + u, s

    s0 = jnp.zeros((B, GLA_HEADS, GLA_HEAD_K, GLA_HEAD_V), f32)
    _, s_prev = lax.scan(step, s0, (jnp.moveaxis(chunk_decay, 2, 0), jnp.moveaxis(chunk_update, 2, 0)))
    s_prev = jnp.moveaxis(s_prev, 0, 2)
    inter = jnp.einsum('bhnik,bhnkv->bhniv', qc * jnp.exp(b_end), s_prev)
    o = _rms_f32(intra + inter, g_out)
    o = o.transpose(0, 2, 3, 1, 4).reshape(B, T, GLA_DV)
    return o * jax.nn.silu(r.astype(f32))


def stick_breaking_mixer(q, k, v, g_q, g_k):
    f32 = jnp.float32
    B, T, _ = q.shape

    def heads(t):
        return t.astype(f32).reshape(B, T, SB_HEADS, SB_HEAD_DIM).transpose(0, 2, 1, 3)

    qh = _rms_f32(heads(q), g_q) * (SB_HEAD_DIM ** -0.5)
    kh = _rms_f32(heads(k), g_k)
    vh = heads(v)
    outs = []
    for blk in range(T // SB_BLOCK):
        q0 = blk * SB_BLOCK
        kv_len = q0 + SB_BLOCK
        past = np.arange(kv_len)[None, :] < np.arange(q0, kv_len)[:, None]
        z = jnp.einsum('bhtd,bhsd->bhts', qh[:, :, q0:kv_len], kh[:, :, :kv_len])
        log_keep = jnp.where(past, jax.nn.log_sigmoid(-z), 0.0)
        between = lax.cumsum(log_keep, axis=3, reverse=True) - log_keep
        w = jnp.where(past, jnp.exp(jax.nn.log_sigmoid(z) + between), 0.0)
        outs.append(jnp.einsum('bhts,bhsd->bhtd', w, vh[:, :, :kv_len]))
    o = jnp.concatenate(outs, axis=2)
    return o.transpose(0, 2, 1, 3).reshape(B, T, SB_D)


def hybrid_mixer(h, w_in, w_gate2, b_gate, g_gla, g_q, g_k, w_out):
    proj = h @ w_in
    cuts = np.cumsum([GLA_DK, GLA_DK, GLA_DV, GLA_DV, GLA_GATE_RANK, SB_D, SB_D])
    gq, gk, gv, gr, glr, sq, sk, sv = jnp.split(proj, cuts, axis=-1)
    o_gla = gla_mixer(gq, gk, gv, gr, glr, w_gate2, b_gate, g_gla)
    o_sb = stick_breaking_mixer(sq, sk, sv, g_q, g_k)
    o = jnp.concatenate([o_gla, o_sb], axis=-1).astype(h.dtype)
    return o @ w_out


def conformer_conv(h, w_pw1, b_pw1, w_dw, b_dw, ln_g, ln_b, w_pw2, b_pw2):
    a = h @ w_pw1 + b_pw1
    u = a[..., :D_MODEL] * jax.nn.sigmoid(a[..., D_MODEL:])
    u = lax.conv_general_dilated(u, w_dw[:, None, :].astype(u.dtype), window_strides=(1,),
                                 padding=[(CONV_WIDTH - 1, 0)],
                                 dimension_numbers=('NWC', 'WIO', 'NWC'),
                                 feature_group_count=D_MODEL) + b_dw
    uf = u.astype(jnp.float32)
    mu = jnp.mean(uf, axis=-1, keepdims=True)
    var = jnp.mean(jnp.square(uf - mu), axis=-1, keepdims=True)
    u = ((uf - mu) * lax.rsqrt(var + EPS) * ln_g + ln_b).astype(h.dtype)
    return jax.nn.silu(u) @ w_pw2 + b_pw2


def swiglu(h, wg, wu, wd):
    return (jax.nn.silu(h @ wg) * (h @ wu)) @ wd


def setup_inputs(seed: int = 0) -> dict:
    key = jax.random.key(seed)
    ks = iter(jax.random.split(key, 32))

    def nrm(shape, scale):
        return jax.random.normal(next(ks), shape, jnp.float32) * scale

    def gain(shape):
        return 1.0 + nrm(shape, 0.05)

    return {
        "x": nrm((BATCH, SEQ, D_MODEL), 1.0),
        "mix_norm": gain((DEPTH, D_MODEL)),
        "ffn_norm": gain((DEPTH, D_MODEL)),
        "hy_w_in": nrm((N_EVEN, D_MODEL, IN_WIDTH), D_MODEL ** -0.5),
        "hy_w_gate2": nrm((N_EVEN, GLA_GATE_RANK, GLA_DK), GLA_GATE_RANK ** -0.5),
        "hy_b_gate": nrm((N_EVEN, GLA_DK), 0.1),
        "hy_gla_norm": gain((N_EVEN, GLA_HEAD_V)),
        "hy_sb_q_norm": gain((N_EVEN, SB_HEAD_DIM)),
        "hy_sb_k_norm": gain((N_EVEN, SB_HEAD_DIM)),
        "hy_w_out": nrm((N_EVEN, MIX_WIDTH, D_MODEL), MIX_WIDTH ** -0.5),
        "cv_w_pw1": nrm((N_ODD, D_MODEL, 2 * D_MODEL), D_MODEL ** -0.5),
        "cv_b_pw1": nrm((N_ODD, 2 * D_MODEL), 0.02),
        "cv_w_dw": nrm((N_ODD, CONV_WIDTH, D_MODEL), CONV_WIDTH ** -0.5),
        "cv_b_dw": nrm((N_ODD, D_MODEL), 0.02),
        "cv_ln_g": gain((N_ODD, D_MODEL)),
        "cv_ln_b": nrm((N_ODD, D_MODEL), 0.02),
        "cv_w_pw2": nrm((N_ODD, D_MODEL, D_MODEL), D_MODEL ** -0.5),
        "cv_b_pw2": nrm((N_ODD, D_MODEL), 0.02),
        "ffn_w_gate": nrm((DEPTH, D_MODEL, D_FF), D_MODEL ** -0.5),
        "ffn_w_up": nrm((DEPTH, D_MODEL, D_FF), D_MODEL ** -0.5),
        "ffn_w_down": nrm((DEPTH, D_FF, D_MODEL), D_FF ** -0.5),
    }


def reference(x, mix_norm, ffn_norm, hy_w_in, hy_w_gate2, hy_b_gate, hy_gla_norm,
              hy_sb_q_norm, hy_sb_k_norm, hy_w_out, cv_w_pw1, cv_b_pw1, cv_w_dw, cv_b_dw,
              cv_ln_g, cv_ln_b, cv_w_pw2, cv_b_pw2, ffn_w_gate, ffn_w_up, ffn_w_down):
    h = x
    for layer in range(DEPTH):
        hn = rms_norm(h, mix_norm[layer])
        if layer % 2 == 0:
            i = layer // 2
            mix = hybrid_mixer(hn, hy_w_in[i], hy_w_gate2[i], hy_b_gate[i], hy_gla_norm[i],
                               hy_sb_q_norm[i], hy_sb_k_norm[i], hy_w_out[i])
        else:
            i = layer // 2
            mix = conformer_conv(hn, cv_w_pw1[i], cv_b_pw1[i], cv_w_dw[i], cv_b_dw[i],
                                 cv_ln_g[i], cv_ln_b[i], cv_w_pw2[i], cv_b_pw2[i])
        h = h + mix.astype(h.dtype)
        h = h + swiglu(rms_norm(h, ffn_norm[layer]), ffn_w_gate[layer], ffn_w_up[layer],
                       ffn_w_down[layer]).astype(h.dtype)
    return h
```

```python
import numpy as np
import ml_dtypes
from contextlib import ExitStack
import concourse.bass as bass
import concourse.mybir as mybir
from concourse.bass_utils import run_bass_kernel_spmd

F32 = mybir.dt.float32
BF16 = mybir.dt.bfloat16
AF = mybir.ActivationFunctionType
ALU = mybir.AluOpType

D = 1024
DFF = 2816
SEQ = 8192
NCORES = 8
TOK = 2048
HALO = 128
NTB = (TOK + HALO) // 128
EPS = 1e-6


class Tile:
    __slots__ = ("t", "name", "lw", "rd", "psum")

    def __init__(self, t, name, psum=False):
        self.t = t
        self.name = name
        self.lw = None
        self.rd = []
        self.psum = psum

    def __getitem__(self, idx):
        return self.t[idx]


class Prog:
    ENGS = ("pe", "act", "dve", "pool", "sp")

    def __init__(self, nc, ndma_sems=14):
        self.nc = nc
        self.es = ExitStack()
        self.eng = {"pe": nc.tensor, "act": nc.scalar, "dve": nc.vector,
                    "pool": nc.gpsimd, "sp": nc.sync}
        self.sem = {}
        self.cnt = {}
        for e in ("pe", "act", "dve", "pool"):
            self.sem[e] = self.es.enter_context(nc.semaphore("c_" + e))
            self.cnt[e] = 0
        self.dq = {}
        self.dsem = {}
        for q in ("sp", "pool"):
            sems = [self.es.enter_context(nc.semaphore("d_%s%d" % (q, i))) for i in range(ndma_sems)]
            self.dq[q] = {"n": ndma_sems, "val": [0] * ndma_sems, "next": 0}
            for i, s in enumerate(sems):
                self.dsem[(q, i)] = s
        self.seen = {e: {} for e in self.ENGS}
        self.ninst = 0
        self.nwaits = 0
        self.pools = []

    def _uid(self):
        self.uid = getattr(self, "uid", 0) + 1
        return self.uid

    def scope(self):
        st = ExitStack()
        self.pools.append(st)
        return st

    def sb(self, name, shape, dt, st=None):
        t = (st or self.es).enter_context(self.nc.sbuf_tensor("s%d_%s" % (self._uid(), name), list(shape), dt))
        return Tile(t, name)

    def ps(self, name, shape, dt=F32, st=None):
        t = (st or self.es).enter_context(self.nc.psum_tensor("p%d_%s" % (self._uid(), name), list(shape), dt))
        return Tile(t, name, psum=True)

    def _wait(self, e, tok):
        if tok is None:
            return
        if tok[0] == "e":
            key = ("e", tok[1])
            sem = self.sem[tok[1]]
            val = tok[2]
        else:
            key = ("d", tok[1])
            sem = self.dsem[tok[1]]
            val = tok[2]
        if self.seen[e].get(key, 0) >= val:
            return
        self.seen[e][key] = val
        self.eng[e].wait_ge(sem, val)
        self.nwaits += 1

    def _deps(self, e, reads, writes):
        for t in reads:
            if t is not None:
                self._wait(e, t.lw)
                if t.psum:
                    for r in t.rd:
                        if not (r[0] == "e" and r[1] == e):
                            self._wait(e, r)
        for t in writes:
            if t is None:
                continue
            lw = t.lw
            if not (e == "pe" and lw is not None and lw[0] == "e" and lw[1] == e):
                self._wait(e, lw)
            for r in t.rd:
                if e == "pe" and r[0] == "e" and r[1] == e:
                    continue
                self._wait(e, r)

    def _mark(self, tok, reads, writes):
        for t in reads:
            if t is not None:
                t.rd = [r for r in t.rd if not (r[0] == tok[0] and r[1] == tok[1])] + [tok]
        for t in writes:
            if t is not None:
                t.lw = tok
                t.rd = []

    def op(self, e, fn, reads=(), writes=()):
        self._deps(e, reads, writes)
        ins = fn(self.eng[e])
        self.cnt[e] += 1
        ins.then_inc(self.sem[e], 1)
        self._mark(("e", e, self.cnt[e]), reads, writes)
        self.ninst += 1
        return ins

    def dma(self, q, out, in_, reads=(), writes=(), **kw):
        d = self.dq[q]
        i = d["next"]
        d["next"] = (i + 1) % d["n"]
        skey = (q, i)
        if d["val"][i] > 0:
            self._wait(q, ("d", skey, d["val"][i]))
        self._deps(q, reads, writes)
        ins = self.eng[q].dma_start(out=out, in_=in_, **kw)
        d["val"][i] += 16
        ins.then_inc(self.dsem[skey], 16)
        tok = ("d", skey, d["val"][i])
        self._mark(tok, reads, writes)
        self.ninst += 1
        return tok

    def collective_allgather(self, in_ap, out_ap, groups, wait_toks):
        for tok in wait_toks:
            self._wait("pool", tok)
        if ("cc", 0) not in self.dsem:
            self.dsem[("cc", 0)] = self.es.enter_context(self.nc.semaphore("cc_sem"))
            self.ncc = 0
        ins = self.nc.gpsimd.collective_compute("AllGather", ALU.bypass, replica_groups=groups,
                                                ins=[in_ap], outs=[out_ap])
        self.ncc += 1
        ins.then_inc(self.dsem[("cc", 0)], 1)
        self.ninst += 1
        return ("d", ("cc", 0), self.ncc)

    def barrier(self):
        for e in self.ENGS:
            for y in ("pe", "act", "dve", "pool"):
                if y != e and self.cnt[y] > 0:
                    self._wait(e, ("e", y, self.cnt[y]))
            for q, d in self.dq.items():
                for i in range(d["n"]):
                    if d["val"][i] > 0:
                        self._wait(e, ("d", (q, i), d["val"][i]))

    def finish(self, out_tiles=()):
        for t in out_tiles:
            self._wait("sp", t.lw)

    def close(self):
        for st in reversed(self.pools):
            st.close()
        self.es.close()


def mm(P, out_t, out_ap, lhsT_t, lhsT_ap, rhs_t, rhs_ap, start, stop, skip=False):
    P.op("pe", lambda e: e.matmul(out_ap, lhsT=lhsT_ap, rhs=rhs_ap, start=start, stop=stop, skip_group_check=skip),
         reads=[lhsT_t, rhs_t], writes=[out_t])


def load_cols(P, q, dst, src_vec, nchunk):
    P.dma(q, dst[:], src_vec.rearrange("(k p) -> p k", p=128), writes=[dst],
          allow_slow_non_contiguous=True)


class Ctx:
    pass


def norm_T(P, C, src_t, src_ap, gcol, dst_t, dst_col0):
    i = C.nrm_i
    C.nrm_i += 1
    hs = C.hs[i % 2]
    pT = C.pT[i % 2]
    ss = C.ss[i % 2]
    P.op("act", lambda e: e.activation(out=C.junk[:], in_=src_ap, func=AF.Square, accum_out=ss[:, 0:1]),
         reads=[src_t], writes=[C.junk, ss])
    P.op("act", lambda e: e.activation(out=ss[:, 1:2], in_=ss[:, 0:1], func=AF.Sqrt, scale=1.0 / D, bias=C.epsc[:, 0:1]),
         reads=[ss, C.epsc], writes=[ss])
    P.op("dve", lambda e: e.reciprocal(out=ss[:, 2:3], in_=ss[:, 1:2]), reads=[ss], writes=[ss])
    P.op("dve", lambda e: e.tensor_scalar(out=hs[:], in0=src_ap, scalar1=ss[:, 2:3], scalar2=None, op0=ALU.mult),
         reads=[src_t, ss], writes=[hs])
    for k in range(8):
        P.op("pe", lambda e: e.transpose(out=pT[:, k, :], in_=hs[:, k * 128:(k + 1) * 128], identity=C.identb[:]),
             reads=[hs, C.identb], writes=[pT])
    for k in range(8):
        P.op("act", lambda e: e.activation(out=dst_t[:, k, dst_col0:dst_col0 + 128], in_=pT[:, k, :], func=AF.Copy,
                                           scale=gcol[:, k:k + 1]),
             reads=[pT, gcol], writes=[dst_t])


def set_gain(P, C, grow_d):
    ones_row = C.wdwc[0:1, 0:4, :].rearrange("p a b -> p (a b)")
    P.dma("sp", ones_row, C.ones_d[0:1, :], writes=[C.wdwc])
    grow = C.tmp[2]
    g2 = grow_d.rearrange("(o n) -> o n", o=1)
    for half in range(2):
        P.dma("sp", grow[0:1, :], g2[0:1, half * 512:(half + 1) * 512], writes=[grow])
        pd = C.pd[half]
        P.op("pe", lambda e: e.matmul(pd[:, :], lhsT=ones_row, rhs=grow[0:1, :], start=True, stop=True), reads=[C.wdwc, grow], writes=[pd])
        P.op("dve", lambda e: e.tensor_copy(out=C.gbc[:, half * 512:(half + 1) * 512], in_=pd[:, :]), reads=[pd], writes=[C.gbc])


def norm_T2_a(P, C, src_t, src_ap):
    i = C.nrm_i
    C.nrm_i += 1
    hs = C.hs[i % 2]
    ss = C.ss[i % 2]
    P.op("act", lambda e: e.activation(out=hs[:], in_=src_ap, func=AF.Square, accum_out=ss[:, 0:1]),
         reads=[src_t], writes=[hs, ss])
    P.op("act", lambda e: e.activation(out=ss[:, 1:2], in_=ss[:, 0:1], func=AF.Sqrt, scale=1.0 / D, bias=C.epsc[:, 0:1]),
         reads=[ss, C.epsc], writes=[ss])
    P.op("dve", lambda e: e.reciprocal(out=ss[:, 2:3], in_=ss[:, 1:2]), reads=[ss], writes=[ss])
    P.op("dve", lambda e: e.scalar_tensor_tensor(out=hs[:], in0=src_ap, scalar=ss[:, 2:3], in1=C.gbc[:], op0=ALU.mult, op1=ALU.mult),
         reads=[src_t, ss, C.gbc], writes=[hs])
    return i


def norm_T2_b(P, C, i, dst_view, dst_tensor, dst_col0):
    hs = C.hs[i % 2]
    pT = C.pT[i % 2]
    pTb = pT[:].bitcast(BF16)
    for k in range(8):
        P.op("pe", lambda e: e.transpose(out=pTb[:, k * 128:(k + 1) * 128], in_=hs[:, k * 128:(k + 1) * 128], identity=C.identb[:]),
             reads=[hs, C.identb], writes=[pT])
    P.op("act", lambda e: e.copy(out=dst_tensor[:, :, dst_col0:dst_col0 + 128], in_=pTb.rearrange("p (k t) -> p k t", k=8)),
         reads=[pT], writes=[dst_view])


def norm_T2(P, C, src_t, src_ap, dst_view, dst_tensor, dst_col0):
    i = norm_T2_a(P, C, src_t, src_ap)
    norm_T2_b(P, C, i, dst_view, dst_tensor, dst_col0)


def tg_of_tile(i):
    return 0 if i == 0 else 1 + (i - 1) // 4


def norm_TA_a(P, C, src_t, src_ap):
    i = C.nrm_i
    C.nrm_i += 1
    hs = C.hs[i % 2]
    ss = C.ss[i % 2]
    P.op("act", lambda e: e.activation(out=hs[:], in_=src_ap, func=AF.Square, accum_out=ss[:, 0:1]),
         reads=[src_t], writes=[hs, ss])
    P.op("act", lambda e: e.activation(out=ss[:, 1:2], in_=ss[:, 0:1], func=AF.Ln, scale=1.0 / D, bias=C.epsc[:, 0:1]),
         reads=[ss, C.epsc], writes=[ss])
    P.op("act", lambda e: e.activation(out=ss[:, 2:3], in_=ss[:, 1:2], func=AF.Exp, scale=-0.5), reads=[ss], writes=[ss])
    P.op("dve", lambda e: e.scalar_tensor_tensor(out=hs[:], in0=src_ap, scalar=ss[:, 2:3], in1=C.gbc[:], op0=ALU.mult, op1=ALU.mult),
         reads=[src_t, ss, C.gbc], writes=[hs])
    return i


def norm_TA_b(P, C, i, dst_t, dst_col0):
    hs = C.hs[i % 2]
    pT = C.pT[i % 2]
    for k in range(8):
        P.op("pe", lambda e: e.transpose(out=pT[:, k, :], in_=hs[:, k * 128:(k + 1) * 128], identity=C.identb[:]),
             reads=[hs, C.identb], writes=[pT])
    P.op("act", lambda e: e.copy(out=dst_t[:, :, dst_col0:dst_col0 + 128], in_=pT[:, :, :]), reads=[pT], writes=[dst_t])


def glu_stage(P, C, hnT, tgs, ngroups, load_w, epilogue, down):
    load_w(0, 0)
    for g in range(ngroups):
        slot = g % 2
        if g + 1 < ngroups:
            load_w(g + 1, (g + 1) % 2)
        hT, hviews = hnT
        for (t0, n) in tgs:
            hvw = hviews[tg_of_tile(t0 // 128)]
            for c in range(2):
                pb = C.gu_i % 2
                C.gu_i += 1
                pg, pu = C.pg[pb], C.pu[pb]
                for k in range(8):
                    mm(P, pg, pg[:, 0:n], C.wg[slot], C.wg[slot][:, k, c * 128:(c + 1) * 128],
                       hvw, hT[:, k, t0:t0 + n], k == 0, k == 7)
                for k in range(8):
                    mm(P, pu, pu[:, 0:n], C.wu[slot], C.wu[slot][:, k, c * 128:(c + 1) * 128],
                       hvw, hT[:, k, t0:t0 + n], k == 0, k == 7)
                epilogue(g, c, t0, n, pg, pu)
            if down is not None:
                down(g, slot, t0, n)


def ffn_layer(P, C, h, hv, hnT, tgs, wg_d, wu_d, wd_d, before_last=None, after_last_down=None):
    NGF = DFF // 256

    def load_w(g, slot):
        f0 = g * 256
        P.dma("pool", C.wg[slot][:], wg_d[:, f0:f0 + 256].rearrange("(k p) n -> p k n", p=128), writes=[C.wg[slot]])
        P.dma("pool", C.wu[slot][:], wu_d[:, f0:f0 + 256].rearrange("(k p) n -> p k n", p=128), writes=[C.wu[slot]])
        P.dma("pool", C.wd[slot][:], wd_d[f0:f0 + 256, :].rearrange("(c p) n -> p c n", p=128), writes=[C.wd[slot]])

    def epilogue(g, c, t0, n, pg, pu):
        sg = C.tmp[C.tmp_i % 2]
        C.tmp_i += 1
        at = C.actT[C.act_i % 2]
        P.op("act", lambda e: e.activation(out=sg[:, 0:n], in_=pg[:, 0:n], func=AF.Silu), reads=[pg], writes=[sg])
        P.op("dve", lambda e: e.tensor_tensor(out=at[:, c, 0:n], in0=sg[:, 0:n], in1=pu[:, 0:n], op=ALU.mult),
             reads=[sg, pu], writes=[at])

    def down(g, slot, t0, n):
        at = C.actT[C.act_i % 2]
        C.act_i += 1
        if g == NGF - 1 and t0 == tgs[0][0] and before_last is not None:
            before_last()
        for j in range(n // 128):
            ti = (t0 + j * 128) // 128
            for half in range(2):
                pd = C.pd4[C.pd_i % 4]
                C.pd_i += 1
                for c in range(2):
                    mm(P, pd, pd[:, :], at, at[:, c, j * 128:(j + 1) * 128],
                       C.wd[slot], C.wd[slot][:, c, half * 512:(half + 1) * 512], c == 0, c == 1)
                P.op("dve", lambda e: e.tensor_tensor(out=h[:, ti, half * 512:(half + 1) * 512],
                                                      in0=h[:, ti, half * 512:(half + 1) * 512], in1=pd[:, :], op=ALU.add),
                     reads=[hv[ti], pd], writes=[hv[ti]])
        if g == NGF - 1 and after_last_down is not None:
            after_last_down(t0, n)

    glu_stage(P, C, hnT, tgs, NGF, load_w, epilogue, down)


def proj_accum(P, C, h, hv, tiles, src, src_col0, wres, bias_row, after_tile=None):
    srcT_tensor, src_views = src
    for idx, ti in enumerate(tiles):
        c0 = src_col0 + idx * 128
        srcT = srcT_tensor
        sv = src_views(c0) if callable(src_views) else src_views
        for half in range(2):
            pd = C.pd[half]
            for k in range(8):
                P.op("pe", lambda e: e.matmul(pd[:, :], lhsT=srcT[:, k, c0:c0 + 128], rhs=wres[:, k, half * 512:(half + 1) * 512],
                                              start=(k == 0), stop=((k == 7) and bias_row is None)),
                     reads=list(sv) + [wres], writes=[pd])
            if bias_row is not None:
                mm(P, pd, pd[:, :], C.onesb, C.onesb[0:1, :], bias_row, bias_row[0:1, half * 512:(half + 1) * 512],
                   False, True)
            P.op("dve", lambda e: e.tensor_tensor(out=h[:, ti, half * 512:(half + 1) * 512],
                                                  in0=h[:, ti, half * 512:(half + 1) * 512], in1=pd[:, :], op=ALU.add),
                 reads=[hv[ti], pd], writes=[hv[ti]])
        if after_tile is not None:
            after_tile(ti)


def phase_b(P, dr, ag=None):
    st = P.scope()
    C = Ctx()
    C.nrm_i = C.gu_i = C.tmp_i = C.act_i = 0
    h = P.sb("h", [128, NTB, D], F32, st)
    hv = [Tile(h.t, "h%d" % i) for i in range(NTB)]
    hnT = P.sb("hnT", [128, 8, NTB * 128], BF16, st)
    oT = P.sb("oTsb", [128, 8, NTB * 128], BF16, st)
    wres = P.sb("wres", [128, 8, D], BF16, st)
    C.wg = [P.sb("wg%d" % i, [128, 8, 256], BF16, st) for i in range(2)]
    C.wu = [P.sb("wu%d" % i, [128, 8, 256], BF16, st) for i in range(2)]
    C.wd = [P.sb("wd%d" % i, [128, 2, D], BF16, st) for i in range(2)]
    C.tmp = [P.sb("tmp%d" % i, [128, 512], F32, st) for i in range(3)]
    C.actT = [P.sb("actT%d" % i, [128, 2, 512], BF16, st) for i in range(2)]
    C.hs = [P.sb("hs%d" % i, [128, D], BF16, st) for i in range(2)]
    C.ss = [P.sb("ss%d" % i, [128, 4], F32, st) for i in range(2)]
    C.gbc = P.sb("gbc", [128, D], F32, st)
    C.ones_d = dr["ones"]
    C.identb = P.sb("identb", [128, 128], BF16, st)
    C.onesb = P.sb("onesb", [128, 128], BF16, st)
    C.epsc = P.sb("epsc", [128, 1], F32, st)
    b1col = P.sb("b1col", [128, 16], F32, st)
    bdwcol = P.sb("bdwcol", [128, 8], F32, st)
    lngcol = P.sb("lngcol", [128, 8], F32, st)
    lnbcol = P.sb("lnbcol", [128, 8], F32, st)
    b2row = P.sb("b2row", [1, D], BF16, st)
    hmask = P.sb("hmask", [128, 1], F32, st)
    wdw = P.sb("wdw", [31, D], BF16, st)
    wdwc = P.sb("wdwc", [128, 8, 32], F32, st)
    C.wdwc = wdwc
    dg = P.sb("dg", [128, 31, 128], BF16, st)
    C.pT = [P.ps("pT%d" % i, [128, 512], F32, st) for i in range(2)]
    C.pg = [P.ps("pg%d" % i, [128, 512], F32, st) for i in range(2)]
    C.pu = [P.ps("pu%d" % i, [128, 512], F32, st) for i in range(2)]
    C.pd = [P.ps("pd%d" % i, [128, 512], F32, st) for i in range(2)]
    C.pd4 = C.pd + C.pT
    C.pd_i = 0
    outd = [Tile(dr["out"], "out%d" % i) for i in range(NTB)]

    P.dma("pool", C.identb[:], dr["ident"][:, :], writes=[C.identb])
    P.dma("pool", C.onesb[:], dr["ones"][:, :], writes=[C.onesb])
    P.dma("sp", C.epsc[:], dr["epsc"][:, :], writes=[C.epsc])
    P.dma("sp", hmask[:], dr["hmask"][:, :], writes=[hmask])
    load_cols(P, "sp", b1col, dr["b_pw1"], 16)
    load_cols(P, "sp", bdwcol, dr["b_dw"], 8)
    load_cols(P, "sp", lngcol, dr["ln_g"], 8)
    load_cols(P, "sp", lnbcol, dr["ln_b"], 8)
    P.dma("pool", b2row[:], dr["b_pw2"].rearrange("(o n) -> o n", o=1), writes=[b2row])
    P.dma("pool", wdw[:], dr["w_dw"][:, :], writes=[wdw])
    P.dma("pool", wres[:], dr["w_out"].rearrange("(k p) n -> p k n", p=128), writes=[wres])
    for i in range(NTB):
        P.dma("sp", h[:, i, :], dr["xh"][i * 128:(i + 1) * 128, :], writes=[hv[i]])
    if ag is None:
        for k in range(8):
            P.dma("sp", oT[:, k, :], dr["oT"][k * 128:(k + 1) * 128, :], writes=[oT])
    else:
        ag_chunks, ag_tok = ag
        sel = P.sb("sel", [128, 4], F32, st)
        P.dma("sp", sel[:], dr["sel"][:, :], writes=[sel])
        dsel = Tile(dg.t, "dsel")
        for j in range(4):
            P.op("dve", lambda e: e.tensor_scalar(out=dg[:, j, :], in0=C.identb[:], scalar1=sel[:, j:j + 1], scalar2=None, op0=ALU.mult),
                 reads=[C.identb, sel], writes=[dsel])
        P._wait("sp", ag_tok)
        P._wait("pool", ag_tok)
        cst = [Tile(hnT.t, "cst%d" % i) for i in range(8)]
        banks = [C.pg[0], C.pg[1], C.pu[0], C.pu[1]]
        nb = 0
        for k in range(8):
            for j in range(4):
                cs = cst[(k % 2) * 4 + j]
                for (ci, off, sw, so) in pad_segments(j * TOK, NTB * 128):
                    P.dma("sp" if j % 2 == 0 else "pool", hnT[:, (k % 2) * 4 + j, so:so + sw],
                          ag_chunks[ci][k * 128:(k + 1) * 128, off:off + sw], writes=[cs])
            for (t0, n) in [(0, 128)] + [(128 + 512 * i, 512) for i in range(4)]:
                pb = banks[nb % 4]
                for j in range(4):
                    cs = cst[(k % 2) * 4 + j]
                    P.op("pe", lambda e: e.matmul(pb[:, 0:n], lhsT=dg[:, j, :], rhs=hnT[:, (k % 2) * 4 + j, t0:t0 + n], start=(j == 0), stop=(j == 3)),
                         reads=[dsel, cs], writes=[pb])
                if nb % 2 == 0:
                    P.op("act", lambda e: e.copy(out=oT[:, k, t0:t0 + n], in_=pb[:, 0:n]), reads=[pb], writes=[oT])
                else:
                    P.op("dve", lambda e: e.tensor_copy(out=oT[:, k, t0:t0 + n], in_=pb[:, 0:n]), reads=[pb], writes=[oT])
                nb += 1
        P.barrier()

    all_tiles = list(range(NTB))
    own_tiles = list(range(1, NTB))
    tgs_all = [(0, 128)] + [(128 + 512 * i, 512) for i in range(4)]
    tgs_own = [(128 + 512 * i, 512) for i in range(4)]

    hn_v = [Tile(hnT.t, "hn_v%d" % i) for i in range(5)]
    oT_v = [Tile(oT.t, "oT_v%d" % i) for i in range(5)]
    P.barrier()

    def norm_tiles(tiles_, dst, dst_views):
        for i in tiles_:
            norm_T2(P, C, hv[i], h[:, i, :], dst_views[tg_of_tile(i)], dst, i * 128)

    class Lagged:
        def __init__(self, dst, dst_views, lag):
            self.q1, self.q2, self.dst, self.dv, self.lag = [], [], dst, dst_views, lag

        def _s1(self):
            tiles_ = self.q1.pop(0)
            self.q2.append([(i, norm_T2_a(P, C, hv[i], h[:, i, :])) for i in tiles_])

        def _s2(self):
            for (i, idx) in self.q2.pop(0):
                norm_T2_b(P, C, idx, self.dv[tg_of_tile(i)], self.dst, i * 128)

        def push(self, tiles_):
            for i in tiles_:
                self.q1.append([i])
                if self.q2:
                    self._s2()
                if len(self.q1) > self.lag:
                    self._s1()

        def flush(self):
            while self.q1 or self.q2:
                if self.q2:
                    self._s2()
                if self.q1:
                    self._s1()

    set_gain(P, C, dr["ffn_norm0"])
    lg0 = Lagged(hnT, hn_v, 1)
    proj_accum(P, C, h, hv, all_tiles, (oT, lambda c0: [oT_v[tg_of_tile(c0 // 128)]]), 0, wres, None,
               after_tile=lambda ti: lg0.push([ti]))
    lg0.flush()

    def conv_gain():
        set_gain(P, C, dr["mix_norm1"])

    lg1 = Lagged(hnT, hn_v, 1)

    def conv_norm(t0, n):
        lg1.push(range(t0 // 128, (t0 + n) // 128))

    ffn_layer(P, C, h, hv, (hnT, hn_v), tgs_all, dr["fg0"], dr["fu0"], dr["fd0"], before_last=conv_gain, after_last_down=conv_norm)
    lg1.flush()

    P.dma("pool", wres[:], dr["w_pw2"].rearrange("(k p) n -> p k n", p=128), writes=[wres])
    uT = oT

    def load_w1(g, slot):
        f0 = g * 256
        P.dma("pool", C.wu[slot][:], dr["w_pw1"][:, f0:f0 + 256].rearrange("(k p) n -> p k n", p=128), writes=[C.wu[slot]])
        P.dma("pool", C.wg[slot][:], dr["w_pw1"][:, D + f0:D + f0 + 256].rearrange("(k p) n -> p k n", p=128), writes=[C.wg[slot]])

    def epi1(g, c, t0, n, pg, pu):
        ch = g * 2 + c
        uv = oT_v[tg_of_tile(t0 // 128)]
        sg = C.tmp[C.tmp_i % 2]
        C.tmp_i += 1
        P.op("act", lambda e: e.activation(out=sg[:, 0:n], in_=pg[:, 0:n], func=AF.Sigmoid, bias=b1col[:, 8 + ch:9 + ch]),
             reads=[pg, b1col], writes=[sg])
        P.op("dve", lambda e: e.scalar_tensor_tensor(out=uT[:, ch, t0:t0 + n], in0=pu[:, 0:n], scalar=b1col[:, ch:ch + 1],
                                                     in1=sg[:, 0:n], op0=ALU.add, op1=ALU.mult),
             reads=[pu, b1col, sg], writes=[uv])
        if t0 == 0:
            P.op("dve", lambda e: e.tensor_scalar(out=uT[:, ch, 0:128], in0=uT[:, ch, 0:128], scalar1=hmask[:, 0:1], scalar2=None,
                                                  op0=ALU.mult), reads=[uv, hmask], writes=[uv])

    glu_stage(P, C, (hnT, hn_v), tgs_all, 4, load_w1, epi1, None)
    for ch in range(8):
        pw = C.pT[ch % 2]
        pwb = pw[:].bitcast(BF16)
        P.op("pe", lambda e: e.transpose(out=pwb[:, 0:31], in_=wdw[0:31, ch * 128:(ch + 1) * 128], identity=C.identb[0:31, 0:31]),
             reads=[wdw, C.identb], writes=[pw])
        P.op("dve", lambda e: e.tensor_copy(out=wdwc[:, ch, 0:31], in_=pwb[:, 0:31]), reads=[pw], writes=[wdwc])
    cvT = hnT
    cv_v = [Tile(hnT.t, "cv_v%d" % i) for i in range(4)]
    dgt = [Tile(dg.t, "dg%d" % j) for j in range(31)]
    pcs = [C.pg[0], C.pg[1], C.pu[0], C.pu[1]]
    for ch in range(8):
        for j in range(31):
            if j % 3 == 2:
                P.op("act", lambda e: e.activation(out=dg[:, j, :], in_=C.identb[:], func=AF.Copy, scale=wdwc[:, ch, j:j + 1]),
                     reads=[C.identb, wdwc], writes=[dgt[j]])
            else:
                P.op("dve", lambda e: e.tensor_scalar(out=dg[:, j, :], in0=C.identb[:], scalar1=wdwc[:, ch, j:j + 1], scalar2=None,
                                                      op0=ALU.mult), reads=[C.identb, wdwc], writes=[dgt[j]])
            for tg in range(4):
                s0 = 128 + 512 * tg - 30 + j
                P.op("pe", lambda e: e.matmul(pcs[tg][:, :], lhsT=dg[:, j, :], rhs=uT[:, ch, s0:s0 + 512], start=(j == 0), stop=(j == 30)),
                     reads=[dgt[j], oT_v[tg], oT_v[tg + 1]], writes=[pcs[tg]])
        for tg in range(4):
            pc = pcs[tg]
            P.op("act", lambda e: e.activation(out=cvT[:, ch, 512 * tg:512 * tg + 512], in_=pc[:, :], func=AF.Identity,
                                               bias=bdwcol[:, ch:ch + 1]), reads=[pc, bdwcol], writes=hn_v + [cv_v[tg]])
    dgf = dg[:].rearrange("p a b -> p (a b)").bitcast(F32)
    ln_mean = [(C.tmp[0], C.tmp[0][:]), (Tile(None, "lnm1"), dgf[:, 0:512])]
    ln_rstd = [(C.tmp[1], C.tmp[1][:]), (Tile(None, "lnr1"), dgf[:, 512:1024])]
    ln_msq = (Tile(None, "lnq"), dgf[:, 1024:1536])
    ln_banks = [(C.pu[0], C.pu[1]), (C.pd[0], C.pd[1])]
    ln_t1 = [(C.tmp[2], C.tmp[2][:]), (C.hs[0], C.hs[0][:].bitcast(F32)), (C.hs[1], C.hs[1][:].bitcast(F32))]

    def ln_stats(tg):
        c0 = 512 * tg
        s1, s2 = ln_banks[tg % 2]
        (mean_t, mean), (rstd_t, rstd) = ln_mean[tg % 2], ln_rstd[tg % 2]
        msq_t, msq = ln_msq
        for ch in range(8):
            P.op("pe", lambda e: e.matmul(s1[:, :], lhsT=C.onesb[:, :], rhs=cvT[:, ch, c0:c0 + 512], start=(ch == 0), stop=(ch == 7)),
                 reads=[C.onesb, cv_v[tg]], writes=[s1])
        for ch in range(8):
            sq = C.actT[ch % 2]
            P.op("act", lambda e: e.activation(out=sq[:, 0, :], in_=cvT[:, ch, c0:c0 + 512], func=AF.Square), reads=[cv_v[tg]], writes=[sq])
            mm(P, s2, s2[:, :], C.onesb, C.onesb[:, :], sq, sq[:, 0, :], ch == 0, ch == 7)
        P.op("act", lambda e: e.activation(out=mean, in_=s1[:, :], func=AF.Copy, scale=1.0 / D), reads=[s1], writes=[mean_t])
        P.op("dve", lambda e: e.tensor_tensor(out=msq, in0=mean, in1=mean, op=ALU.mult), reads=[mean_t], writes=[msq_t])
        P.op("dve", lambda e: e.scalar_tensor_tensor(out=rstd, in0=s2[:, :], scalar=1.0 / D, in1=msq, op0=ALU.mult, op1=ALU.subtract),
             reads=[s2, msq_t], writes=[rstd_t])
        P.op("act", lambda e: e.activation(out=rstd, in_=rstd, func=AF.Sqrt, bias=C.epsc[:, 0:1]), reads=[rstd_t, C.epsc], writes=[rstd_t])
        P.op("dve", lambda e: e.reciprocal(out=rstd, in_=rstd), reads=[rstd_t], writes=[rstd_t])

    def ln_apply(tg):
        c0 = 512 * tg
        (mean_t, mean), (rstd_t, rstd) = ln_mean[tg % 2], ln_rstd[tg % 2]
        for ch in range(8):
            t1_t, t1 = ln_t1[ch % 3]
            P.op("dve", lambda e: e.tensor_tensor(out=t1, in0=cvT[:, ch, c0:c0 + 512], in1=mean, op=ALU.subtract),
                 reads=[cv_v[tg], mean_t], writes=[t1_t])
            P.op("dve", lambda e: e.tensor_tensor(out=t1, in0=t1, in1=rstd, op=ALU.mult), reads=[t1_t, rstd_t], writes=[t1_t])
            P.op("act", lambda e: e.activation(out=cvT[:, ch, c0:c0 + 512], in_=t1, func=AF.Silu, scale=lngcol[:, ch:ch + 1],
                                               bias=lnbcol[:, ch:ch + 1]), reads=[t1_t, lngcol, lnbcol], writes=[cv_v[tg]])

    set_gain(P, C, dr["ffn_norm1"])
    lg2 = Lagged(oT, oT_v, 1)
    ln_stats(0)
    for tg in range(4):
        if tg + 1 < 4:
            ln_stats(tg + 1)
        ln_apply(tg)
        proj_accum(P, C, h, hv, own_tiles[4 * tg:4 * tg + 4], (cvT, lambda c0: [cv_v[c0 // 512]]), 512 * tg, wres, b2row,
                   after_tile=lambda ti: lg2.push([ti]))
        lg2.flush()
    ffn_layer(P, C, h, hv, (oT, oT_v), tgs_own, dr["fg1"], dr["fu1"], dr["fd1"])
    for i in own_tiles:
        P.dma("sp", dr["out"][(i - 1) * 128:i * 128, :], h[:, i, :], reads=[hv[i]], writes=[outd[i]])
    return outd[1:]


AGC = 1664
NAGC = (HALO + SEQ) // AGC


def pad_segments(pc0, w):
    segs = []
    c = pc0
    while c < pc0 + w:
        ci, off = c // AGC, c % AGC
        sw = min(AGC - off, pc0 + w - c)
        segs.append((ci, off, sw, c - pc0))
        c += sw
    return segs


def dram_in(nc, name, shape, dt=F32):
    return nc.dram_tensor(name, list(shape), dt, kind="ExternalInput").ap()


B_INPUTS = [("xh", [TOK + HALO, D], F32), ("oT", [D, TOK + HALO], BF16), ("w_out", [D, D], F32),
            ("fg0", [D, DFF], F32), ("fu0", [D, DFF], F32), ("fd0", [DFF, D], F32),
            ("fg1", [D, DFF], F32), ("fu1", [D, DFF], F32), ("fd1", [DFF, D], F32),
            ("ffn_norm0", [D], F32), ("ffn_norm1", [D], F32), ("mix_norm1", [D], F32),
            ("w_pw1", [D, 2 * D], F32), ("b_pw1", [2 * D], F32), ("w_dw", [31, D], F32), ("b_dw", [D], F32),
            ("ln_g", [D], F32), ("ln_b", [D], F32), ("w_pw2", [D, D], F32), ("b_pw2", [D], F32),
            ("hmask", [128, 1], F32), ("ident", [128, 128], F32), ("ones", [128, 128], F32), ("epsc", [128, 1], F32)]


def build_b():
    nc = bass.Bass("TRN2", target_bir_lowering=False)
    dr = {n: dram_in(nc, n, s, dt) for n, s, dt in B_INPUTS}
    dr["out"] = nc.dram_tensor("out", [TOK, D], F32, kind="ExternalOutput").ap()
    P = Prog(nc)
    outd = phase_b(P, dr)
    P.finish(outd)
    P.close()
    return nc


def b_in_maps(inp, oT_full):
    x = inp["x"]
    maps = []
    for c in range(NCORES):
        b, j = c // 4, c % 4
        t0 = j * TOK
        xh = np.zeros((TOK + HALO, D), np.float32)
        xh[HALO:] = x[b, t0:t0 + TOK]
        if j > 0:
            xh[:HALO] = x[b, t0 - HALO:t0]
        oT = None
        if oT_full is not None:
            oT = np.zeros((D, TOK + HALO), oT_full.dtype)
            oT[:, HALO:] = oT_full[b][:, t0:t0 + TOK]
            if j > 0:
                oT[:, :HALO] = oT_full[b][:, t0 - HALO:t0]
        m = {"xh": xh, "oT": oT, "w_out": inp["hy_w_out"][0],
             "fg0": inp["ffn_w_gate"][0], "fu0": inp["ffn_w_up"][0], "fd0": inp["ffn_w_down"][0],
             "fg1": inp["ffn_w_gate"][1], "fu1": inp["ffn_w_up"][1], "fd1": inp["ffn_w_down"][1],
             "ffn_norm0": inp["ffn_norm"][0], "ffn_norm1": inp["ffn_norm"][1], "mix_norm1": inp["mix_norm"][1],
             "w_pw1": inp["cv_w_pw1"][0], "b_pw1": inp["cv_b_pw1"][0], "w_dw": inp["cv_w_dw"][0], "b_dw": inp["cv_b_dw"][0],
             "ln_g": inp["cv_ln_g"][0], "ln_b": inp["cv_ln_b"][0], "w_pw2": inp["cv_w_pw2"][0], "b_pw2": inp["cv_b_pw2"][0],
             "hmask": np.full((128, 1), 1.0 if j > 0 else 0.0, np.float32),
             "ident": np.eye(128, dtype=np.float32), "ones": np.ones((128, 128), np.float32),
             "epsc": np.full((128, 1), EPS, np.float32)}
        maps.append({k: np.ascontiguousarray(v) for k, v in m.items() if v is not None})
    return maps


NWA = 848


DBG = 99
SCHED = [1, 1, 2]


def phase_a(P, dr, ntiles=SEQ // 128):
    st = P.scope()
    C = Ctx()
    C.nrm_i = 0
    NT = ntiles
    NG = NT // 4
    T = NT * 128
    C.identb = P.sb("identb", [128, 128], BF16, st)
    C.onesb = P.sb("onesb", [128, 128], BF16, st)
    onesf = P.sb("onesf", [128, 128], F32, st)
    C.epsc = P.sb("epsc", [128, 1], F32, st)
    Dm = P.sb("Dm", [128, 128], F32, st)
    Bm = P.sb("Bm", [128, 128], F32, st)
    BD = P.sb("BD", [128, 128], F32, st)
    Ms = P.sb("Ms", [128, 128], BF16, st)
    TriN = P.sb("TriN", [128, 128], BF16, st)
    OnesN = P.sb("OnesN", [128, 128], BF16, st)
    for t_, n_, q_ in ((C.identb, "ident", "pool"), (C.onesb, "ones", "pool"), (onesf, "ones", "sp"), (C.epsc, "epsc", "sp"),
                       (Dm, "Dm", "sp"), (Bm, "Bm", "sp"), (BD, "BD", "sp"), (Ms, "Ms", "pool"), (TriN, "TriN", "pool"),
                       (OnesN, "OnesN", "pool")):
        P.dma(q_, t_[:], dr[n_][:, :], writes=[t_])
    w_sb = P.sb("w_sb", [128, 8, NWA], BF16, st)
    P.dma("pool", w_sb[:], dr["w_in"].rearrange("(k p) n -> p k n", p=128), writes=[w_sb])
    wg2 = P.sb("wg2", [16, 64], BF16, st)
    P.dma("pool", wg2[:], dr["w_gate2"][:, :], writes=[wg2])
    bgrow = P.sb("bgrow", [1, 64], BF16, st)
    P.dma("pool", bgrow[:], dr["b_gate"].rearrange("(o n) -> o n", o=1), writes=[bgrow])
    gnrow = P.sb("gnrow", [1, 128], F32, st)
    P.dma("sp", gnrow[:], dr["gla_norm"].rearrange("(o n) -> o n", o=1), writes=[gnrow])
    gqc = P.sb("gqc", [128, 2], F32, st)
    gkc = P.sb("gkc", [128, 1], F32, st)
    for hh_ in range(2):
        P.dma("sp", gqc[64 * hh_:64 * hh_ + 64, 0:1], dr["q_norm"].rearrange("(p o) -> p o", o=1), writes=[gqc])
        P.dma("sp", gkc[64 * hh_:64 * hh_ + 64, 0:1], dr["k_norm"].rearrange("(p o) -> p o", o=1), writes=[gkc])
    P.op("act", lambda e: e.mul(out=gqc[:, 1:2], in_=gqc[:, 0:1], mul=0.125), reads=[gqc], writes=[gqc])
    gcol0 = P.sb("gcol0", [128, 8], F32, st)
    load_cols(P, "sp", gcol0, dr["mix_norm0"], 8)
    qnT2 = P.sb("qnT2", [128, T], BF16, st)
    knT2 = P.sb("knT2", [128, T], BF16, st)
    BDb = P.sb("BDb", [128, 128], BF16, st)
    P.dma("pool", BDb[:], dr["BD"][:, :], writes=[BDb])
    vsb = P.sb("vsb", [128, NT, 128], BF16, st)
    gstage = [P.sb("gstage%d" % i, [128, 512], BF16, st) for i in range(2)]
    ostage = [P.sb("ostage%d" % i, [64, 512], BF16, st) for i in range(2)]
    outs = []
    xst = [P.sb("xst%d" % i, [128, D], F32, st) for i in range(2)]
    C.hs = [P.sb("hs%d" % i, [128, D], BF16, st) for i in range(2)]
    C.ss = [P.sb("ss%d" % i, [128, 4], F32, st) for i in range(2)]
    C.gbc = P.sb("gbcA", [128, D], F32, st)
    hnTg = [P.sb("hnTg%d" % i, [128, 8, 512], BF16, st) for i in range(2)]
    gqT = [P.sb("gqT%d" % i, [64, 512], BF16, st) for i in range(2)]
    gkT = [P.sb("gkT%d" % i, [64, 512], F32, st) for i in range(2)]
    glrT = [P.sb("glrT%d" % i, [16, 512], BF16, st) for i in range(2)]
    sqv = [P.sb("sqv%d" % i, [128, 512], BF16, st) for i in range(2)]
    sdt = [P.sb("sdt%d" % i, [128, 512], F32, st) for i in range(2)]
    gno = P.sb("gno", [128, 128], F32, st)
    BD4 = P.sb("BD4", [128, 4, 128], F32, st)
    vg = [P.sb("vg%d" % i, [128, 4, 128], BF16, st) for i in range(2)]
    sr = [P.sb("sr%d" % i, [128, 4, 128], F32, st) for i in range(2)]
    gktm = [P.sb("gktm%d" % i, [128, 4, 64], F32, st) for i in range(2)]
    e1 = P.sb("e1", [128, 4, 64], F32, st)
    lt = P.sb("lt", [128, 4, 64], F32, st)
    kdT = P.sb("kdT", [64, 4, 128], F32, st)
    qdT = P.sb("qdT", [64, 4, 128], F32, st)
    kdt = P.sb("kdt", [128, 4, 64], F32, st)
    kendT = P.sb("kendT", [64, 4, 128], BF16, st)
    qe0 = P.sb("qe0", [64, 4, 128], BF16, st)
    qe1 = P.sb("qe1", [64, 4, 128], BF16, st)
    ke0 = P.sb("ke0", [128, 4, 64], BF16, st)
    ke1 = P.sb("ke1", [128, 4, 64], BF16, st)
    scT = P.sb("scT", [128, 4, 128], BF16, st)
    NSR = 10
    Sf = [P.sb("Sf%d" % i, [64, 128], F32, st) for i in range(NSR)]
    Sb = [P.sb("Sb%d" % i, [64, 128], BF16, st) for i in range(NSR)]
    og = P.sb("og", [128, 4, 128], F32, st)
    ofin = P.sb("ofin", [128, 4, 128], BF16, st)
    ssg = P.sb("ssg", [128, 16], F32, st)
    Ef = [P.sb("Ef%d" % i, [128, 2, 512], F32, st) for i in range(2)]
    Lb = [P.sb("Lb%d" % i, [128, 2, 512], BF16, st) for i in range(3)]
    Wb = [P.sb("Wb%d" % i, [128, 2, 512], BF16, st) for i in range(2)]
    Rf = P.sb("Rf", [128, 2, 512], F32, st)
    Rb = [P.sb("Rb%d" % i, [128, 2, 512], BF16, st) for i in range(4)]
    pst = ExitStack()
    bank = [P.ps("bank%d" % i, [128, 512], F32, pst) for i in range(6)]
    C.pT = [P.ps("pT%d" % i, [128, 8, 128], BF16, pst) for i in range(2)]
    F0, F1, F2, GA, GB, GC = bank
    fa = F0

    for t_ in (qe0, qe1, ke0, ke1):
        P.op("pool", lambda e: e.memset(t_[:], 0.0), writes=[t_])
    P.op("pool", lambda e: e.memset(Sf[0][:], 0.0), writes=[Sf[0]])
    P.op("pool", lambda e: e.memset(Sb[0][:], 0.0), writes=[Sb[0]])
    for jt_ in range(4):
        P.dma("sp", BD4[:, jt_, :], dr["BD"][:, :], writes=[BD4])
    mm(P, fa, fa[:, 0:128], onesf, onesf[0:1, :], gnrow, gnrow[0:1, :], True, True)
    P.op("dve", lambda e: e.tensor_copy(out=gno[:], in_=fa[:, 0:128]), reads=[fa], writes=[gno])
    P.dma("sp", xst[0][0:1, :], dr["mix_norm0"].rearrange("(o n) -> o n", o=1), writes=[xst[0]])
    for half in range(2):
        mm(P, fa, fa[:, :], onesf, onesf[0:1, :], xst[0], xst[0][0:1, half * 512:(half + 1) * 512], True, True)
        P.op("dve", lambda e: e.tensor_copy(out=C.gbc[:, half * 512:(half + 1) * 512], in_=fa[:, :]), reads=[fa], writes=[C.gbc])

    def emit_out(src, r0, nr, c0):
        if "ag_chunks" in dr:
            for (ci, off, sw, so) in pad_segments(HALO + c0, 512):
                tl = Tile(None, "ago")
                P.dma("sp", dr["ag_chunks"][ci][r0:r0 + nr, off:off + sw], src[:, so:so + sw], reads=[src], writes=[tl])
                outs.append((ci, tl))
        else:
            tl = Tile(None, "oTo")
            P.dma("sp", dr["oT_out"][r0:r0 + nr, c0:c0 + 512], src[:, :], reads=[src], writes=[tl])
            outs.append(tl)

    xd = dr["x"]

    def nrm(G):
        hg = hnTg[G % 2]
        pend = None
        for jt in range(4):
            i = 4 * G + jt
            xs = xst[i % 2]
            P.dma("sp", xs[:], xd[i * 128:(i + 1) * 128, :], writes=[xs])
            if pend is not None:
                norm_TA_b(P, C, pend[0], hg, pend[1])
            pend = (norm_TA_a(P, C, xs, xs[:]), jt * 128)
            yield
        norm_TA_b(P, C, pend[0], hg, pend[1])
        yield

    def fmtm(G):
        hg = hnTg[G % 2]
        gp = G % 2
        fmb = [F0, F1]
        nfm = [0]

        def proj(col0, m):
            b = fmb[nfm[0] % 2]
            nfm[0] += 1
            for k in range(8):
                mm(P, b, b[0:m, :], w_sb, w_sb[:, k, col0:col0 + m], hg, hg[:, k, :], k == 0, k == 7)
            return b
        b = proj(0, 64)
        P.op("act", lambda e: e.mul(out=gqT[gp][:], in_=b[0:64, :], mul=0.125), reads=[b], writes=[gqT[gp]])
        yield
        b = proj(64, 64)
        P.op("dve", lambda e: e.tensor_copy(out=gkT[gp][:], in_=b[0:64, :]), reads=[b], writes=[gkT[gp]])
        yield
        b = proj(128, 16)
        P.op("dve", lambda e: e.tensor_copy(out=glrT[gp][:], in_=b[0:16, :]), reads=[b], writes=[glrT[gp]])
        yield
        items = [(144, gqc[:, 1:2], gqc, qnT2), (272, gkc[:, 0:1], gkc, knT2)]
        for n_, (col0, gc_ap, gc_t, dst_t) in enumerate(items):
            b = proj(col0, 128)
            sq_, sd_ = sqv[n_ % 2], sdt[n_ % 2]
            P.op("act", lambda e: e.activation(out=sq_[:], in_=b[:, :], func=AF.Square), reads=[b], writes=[sq_])
            yield
            mm(P, F2, F2[:, :], BDb, BDb[:, :], sq_, sq_[:], True, True)
            P.op("act", lambda e: e.activation(out=sd_[:], in_=F2[:, :], func=AF.Ln, scale=1.0 / 64, bias=C.epsc[:, 0:1]),
                 reads=[F2, C.epsc], writes=[sd_])
            P.op("act", lambda e: e.activation(out=sd_[:], in_=sd_[:], func=AF.Exp, scale=-0.5), reads=[sd_], writes=[sd_])
            P.op("dve", lambda e: e.scalar_tensor_tensor(out=dst_t[:, 512 * G:512 * G + 512], in0=b[:, :], scalar=gc_ap, in1=sd_[:],
                                                         op0=ALU.mult, op1=ALU.mult), reads=[b, gc_t, sd_], writes=[dst_t])
            yield
        for jt in range(4):
            i = 4 * G + jt
            cg = jt * 128
            pB = fmb[jt % 2]
            for k in range(8):
                mm(P, pB, pB[:, 0:448], hg, hg[:, k, cg:cg + 128], w_sb, w_sb[:, k, 400:848], k == 0, k == 7)
            P.op("dve", lambda e: e.tensor_copy(out=vg[gp][:, jt, :], in_=pB[:, 0:128]), reads=[pB], writes=[vg[gp]])
            P.op("act", lambda e: e.activation(out=sr[gp][:, jt, :], in_=pB[:, 128:256], func=AF.Silu), reads=[pB], writes=[sr[gp]])
            P.op("act", lambda e: e.copy(out=vsb[:, i, :], in_=pB[:, 256:384]), reads=[pB], writes=[vsb])
            P.op("dve", lambda e: e.tensor_copy(out=gktm[gp][:, jt, :], in_=pB[:, 384:448]), reads=[pB], writes=[gktm[gp]])
            yield

    def gla(G):
        gp = G % 2
        for jt in range(4):
            mm(P, GA, GA[:, jt * 64:(jt + 1) * 64], glrT[gp], glrT[gp][:, jt * 128:(jt + 1) * 128], wg2, wg2[:, :], True, False)
            mm(P, GA, GA[:, jt * 64:(jt + 1) * 64], C.onesb, C.onesb[0:1, :], bgrow, bgrow[0:1, :], False, True)
        yield
        P.op("act", lambda e: e.activation(out=e1[:], in_=GA[:, 0:256], func=AF.Exp, scale=-1.0), reads=[GA], writes=[e1])
        P.op("act", lambda e: e.activation(out=lt[:], in_=e1[:], func=AF.Ln, bias=1.0), reads=[e1], writes=[lt])
        yield
        for jt in range(4):
            mm(P, GB, GB[0:64, jt * 128:(jt + 1) * 128], lt, lt[:, jt, :], Dm, Dm[:], True, True)
        for jt in range(4):
            mm(P, GC, GC[0:64, jt * 128:(jt + 1) * 128], lt, lt[:, jt, :], Bm, Bm[:], True, True)
        for jt in range(4):
            mm(P, GA, GA[:, 256 + jt * 64:256 + (jt + 1) * 64], Dm, Dm[:], lt, lt[:, jt, :], True, True)
        yield
        P.op("act", lambda e: e.activation(out=kdT[:], in_=GB[0:64, :], func=AF.Exp), reads=[GB], writes=[kdT])
        P.op("act", lambda e: e.activation(out=qdT[:], in_=GC[0:64, :], func=AF.Exp), reads=[GC], writes=[qdT])
        P.op("act", lambda e: e.activation(out=kdt[:], in_=GA[:, 256:512], func=AF.Exp), reads=[GA], writes=[kdt])
        yield
        P.op("dve", lambda e: e.tensor_tensor(out=kendT[:], in0=gkT[gp][:], in1=kdT[:], op=ALU.mult), reads=[gkT[gp], kdT], writes=[kendT])
        gq3 = gqT[gp][:].rearrange("p (j t) -> p j t", j=4)
        P.op("dve", lambda e: e.tensor_tensor(out=qe0[:, :, 0:64], in0=gq3[:, :, 0:64], in1=qdT[:, :, 0:64], op=ALU.mult),
             reads=[gqT[gp], qdT], writes=[qe0])
        P.op("dve", lambda e: e.tensor_tensor(out=qe1[:, :, 64:128], in0=gq3[:, :, 64:128], in1=qdT[:, :, 64:128], op=ALU.mult),
             reads=[gqT[gp], qdT], writes=[qe1])
        P.op("dve", lambda e: e.tensor_tensor(out=ke0[0:64, :, :], in0=gktm[gp][0:64, :, :], in1=kdt[0:64, :, :], op=ALU.mult),
             reads=[gktm[gp], kdt], writes=[ke0])
        P.op("dve", lambda e: e.tensor_tensor(out=ke1[64:128, :, :], in0=gktm[gp][64:128, :, :], in1=kdt[64:128, :, :], op=ALU.mult),
             reads=[gktm[gp], kdt], writes=[ke1])
        yield
        for jt in range(4):
            mm(P, GB, GB[:, jt * 128:(jt + 1) * 128], kendT, kendT[:, jt, :], gqT[gp], gqT[gp][:, jt * 128:(jt + 1) * 128], True, True)
        yield
        P.op("dve", lambda e: e.tensor_tensor(out=scT[:], in0=GB[:, :], in1=BD4[:], op=ALU.mult), reads=[GB, BD4], writes=[scT])
        yield
        for jt in range(4):
            mm(P, GC, GC[0:64, jt * 128:(jt + 1) * 128], ke0, ke0[:, jt, :], vg[gp], vg[gp][:, jt, :], True, True)
        yield
        for jt in range(4):
            mm(P, GB, GB[0:64, jt * 128:(jt + 1) * 128], ke1, ke1[:, jt, :], vg[gp], vg[gp][:, jt, :], True, True)
        yield
        for jt in range(4):
            for half, ub in ((0, GC), (1, GB)):
                c = 8 * G + 2 * jt + half
                s_in, s_out = Sf[c % NSR], Sf[(c + 1) % NSR]
                P.op("dve", lambda e: e.scalar_tensor_tensor(out=s_out[:], in0=s_in[:], scalar=qdT[:, jt, 64 * half:64 * half + 1],
                                                             in1=ub[0:64, jt * 128:(jt + 1) * 128], op0=ALU.mult, op1=ALU.add),
                     reads=[s_in, qdT, ub], writes=[s_out])
                sb_o = Sb[(c + 1) % NSR]
                P.op("pool", lambda e: e.tensor_copy(out=sb_o[:], in_=s_out[:]), reads=[s_out], writes=[sb_o])
        yield
        for jt in range(4):
            c = 8 * G + 2 * jt
            oo = GA[:, jt * 128:(jt + 1) * 128]
            mm(P, GA, oo, scT, scT[:, jt, :], vg[gp], vg[gp][:, jt, :], True, False)
            mm(P, GA, oo, qe0, qe0[:, jt, :], Sb[c % NSR], Sb[c % NSR][:], False, False)
            mm(P, GA, oo, qe1, qe1[:, jt, :], Sb[(c + 1) % NSR], Sb[(c + 1) % NSR][:], False, True)
        yield
        for jt in range(4):
            P.op("act", lambda e: e.activation(out=ofin[:, jt, :], in_=GA[:, jt * 128:(jt + 1) * 128], func=AF.Square,
                                               accum_out=ssg[:, jt:jt + 1]), reads=[GA], writes=[ofin, ssg])
        P.op("act", lambda e: e.activation(out=ssg[:, 4:8], in_=ssg[:, 0:4], func=AF.Ln, scale=1.0 / 128, bias=C.epsc[:, 0:1]),
             reads=[ssg, C.epsc], writes=[ssg])
        P.op("act", lambda e: e.activation(out=ssg[:, 8:12], in_=ssg[:, 4:8], func=AF.Exp, scale=-0.5), reads=[ssg], writes=[ssg])
        for jt in range(4):
            P.op("dve", lambda e: e.scalar_tensor_tensor(out=og[:, jt, :], in0=GA[:, jt * 128:(jt + 1) * 128], scalar=ssg[:, 8 + jt:9 + jt],
                                                         in1=gno[:], op0=ALU.mult, op1=ALU.mult), reads=[GA, ssg, gno], writes=[og])
        P.op("dve", lambda e: e.tensor_tensor(out=ofin[:], in0=og[:], in1=sr[gp][:], op=ALU.mult), reads=[og, sr[gp]], writes=[ofin])
        yield
        pt = C.pT[G % 2]
        for jt in range(4):
            P.op("pe", lambda e: e.transpose(out=pt[:, jt, :], in_=ofin[:, jt, :], identity=C.identb[:]), reads=[ofin, C.identb], writes=[pt])
        P.op("act", lambda e: e.copy(out=gstage[gp][:].rearrange("p (j t) -> p j t", j=4), in_=pt[:, 0:4, :]), reads=[pt], writes=[gstage[gp]])
        emit_out(gstage[gp], 0, 128, 512 * G)
        yield

    def step(gen):
        try:
            next(gen)
            return True
        except StopIteration:
            return False

    for G in range(NG + 2):
        gn = nrm(G) if G < NG else iter(())
        gpj = fmtm(G - 1) if 1 <= G <= NG else iter(())
        gg = gla(G - 2) if G >= 2 else iter(())
        alive = [True, True, True]
        rnd = 0
        while any(alive):
            for _ in range(SCHED[0]):
                if alive[1]:
                    alive[1] = step(gpj)
            for _ in range(SCHED[1]):
                if alive[2]:
                    alive[2] = step(gg)
            if alive[0] and (rnd % SCHED[2] == 0 or not (alive[1] or alive[2])):
                alive[0] = step(gn)
            rnd += 1

    P.barrier()
    pst.close()
    pst2 = ExitStack()
    P.pools.append(pst2)
    pz = [P.ps("pz%d" % i, [128, 2, 512], F32, pst2) for i in range(3)]
    pob = [P.ps("pob%d" % i, [128, 512], F32, pst2) for i in range(2)]
    units = []
    for G in range(NG):
        for kt in range(4 * G + 3, -1, -1):
            units.append((G, kt))

    def geom(G, kt):
        j = kt - 4 * G
        diag = j >= 0
        c0 = 128 * j if diag else 0
        cc0 = c0 + 128 if diag else 0
        return diag, c0, cc0

    def stage1(u):
        G, kt = units[u]
        diag, c0, cc0 = geom(G, kt)
        q0 = 512 * G
        z, Et, Lt_ = pz[u % 3], Ef[u % 2], Lb[u % 3]
        for hh in range(2):
            mm(P, z, z[:, hh, c0:512], knT2, knT2[64 * hh:64 * hh + 64, kt * 128:(kt + 1) * 128],
               qnT2, qnT2[64 * hh:64 * hh + 64, q0 + c0:q0 + 512], True, False, True)
        P.op("act", lambda e: e.activation(out=Et[:, :, c0:512], in_=z[:, :, c0:512], func=AF.Exp), reads=[z], writes=[Et])
        P.op("act", lambda e: e.activation(out=Lt_[:, :, c0:512], in_=Et[:, :, c0:512], func=AF.Ln, bias=1.0), reads=[Et], writes=[Lt_])
        if diag:
            for hh in range(2):
                P.op("dve", lambda e: e.tensor_tensor(out=Lt_[:, hh, c0:c0 + 128], in0=Lt_[:, hh, c0:c0 + 128], in1=Ms[:], op=ALU.mult),
                     reads=[Lt_, Ms], writes=[Lt_])
            P.op("dve", lambda e: e.tensor_copy(out=Rf[:, :, c0:c0 + 128], in_=Lt_[:, :, c0:c0 + 128]), reads=[Lt_], writes=[Rf])
        if cc0 < 512:
            P.op("dve", lambda e: e.tensor_tensor(out=Rf[:, :, cc0:512], in0=Rf[:, :, cc0:512], in1=Lt_[:, :, cc0:512], op=ALU.add),
                 reads=[Rf, Lt_], writes=[Rf])
        rb = Rb[(u + 1) % 4]
        P.op("dve", lambda e: e.tensor_copy(out=rb[:, :, c0:512], in_=Rf[:, :, c0:512]), reads=[Rf], writes=[rb])

    def stage2(u):
        G, kt = units[u]
        diag, c0, cc0 = geom(G, kt)
        z, Lt_, Wt, rb = pz[u % 3], Lb[u % 3], Wb[u % 2], Rb[u % 4]
        for hh in range(2):
            mm(P, z, z[:, hh, c0:512], TriN, TriN[:], Lt_, Lt_[:, hh, c0:512], False, cc0 >= 512, True)
            if cc0 < 512:
                mm(P, z, z[:, hh, cc0:512], OnesN, OnesN[:], rb, rb[:, hh, cc0:512], False, True, True)
        P.op("act", lambda e: e.activation(out=Wt[:, :, c0:512], in_=z[:, :, c0:512], func=AF.Exp), reads=[z], writes=[Wt])
        if diag:
            for hh in range(2):
                P.op("dve", lambda e: e.tensor_tensor(out=Wt[:, hh, c0:c0 + 128], in0=Wt[:, hh, c0:c0 + 128], in1=Ms[:], op=ALU.mult),
                     reads=[Wt, Ms], writes=[Wt])

    def stage3(u):
        G, kt = units[u]
        diag, c0, cc0 = geom(G, kt)
        q0 = 512 * G
        Wt = Wb[u % 2]
        first = kt == 4 * G + 3
        last = kt == 0
        for hh in range(2):
            o = pob[hh]
            vv = vsb[:, kt, hh * 64:(hh + 1) * 64]
            mm(P, o, o[0:64, c0:512], vsb, vv, Wt, Wt[:, hh, c0:512], first, last, True)
            if last:
                og_ = ostage[hh]
                P.op("dve", lambda e: e.tensor_copy(out=og_[:], in_=o[0:64, :]), reads=[o], writes=[og_])
                emit_out(og_, 128 + 64 * hh, 64, q0)
                if hh == 1 and "ag_ready" in dr:
                    dr["ag_ready"](q0 + 512, outs)

    nu = len(units)
    if DBG <= 6:
        nu = 0
    for u in range(min(2, nu)):
        stage1(u)
    for u in range(nu):
        stage2(u)
        if u + 2 < nu:
            stage1(u + 2)
        if u >= 1:
            stage3(u - 1)
    if nu:
        stage3(nu - 1)
    return outs


A_INPUTS = [("x", [SEQ, D], F32), ("w_in", [D, NWA], F32), ("w_gate2", [16, 64], F32), ("b_gate", [64], F32),
            ("gla_norm", [128], F32), ("q_norm", [64], F32), ("k_norm", [64], F32), ("mix_norm0", [D], F32),
            ("ident", [128, 128], F32), ("ones", [128, 128], F32), ("epsc", [128, 1], F32),
            ("Dm", [128, 128], F32), ("Bm", [128, 128], F32), ("BD", [128, 128], F32), ("Ms", [128, 128], F32),
            ("TriN", [128, 128], F32), ("OnesN", [128, 128], F32)]


def build_a(ntiles=SEQ // 128):
    nc = bass.Bass("TRN2", target_bir_lowering=False)
    dr = {}
    for n, s, dt in A_INPUTS:
        if n == "x":
            s = [ntiles * 128, D]
        dr[n] = dram_in(nc, n, s, dt)
    dr["oT_out"] = nc.dram_tensor("oT_out", [256, ntiles * 128], BF16, kind="ExternalOutput").ap()
    P = Prog(nc)
    outs = phase_a(P, dr, ntiles)
    P.finish(outs)
    P.close()
    return nc


def a_consts():
    t = np.arange(128)
    same = (t[:, None] // 64) == (t[None, :] // 64)
    c = {"ident": np.eye(128, dtype=np.float32), "ones": np.ones((128, 128), np.float32),
         "epsc": np.full((128, 1), EPS, np.float32),
         "Dm": np.where(same & (t[:, None] > t[None, :]), -1.0 / 16, 0.0).astype(np.float32),
         "Bm": np.where(same, -1.0 / 16, 0.0).astype(np.float32),
         "BD": same.astype(np.float32),
         "Ms": (t[:, None] < t[None, :]).astype(np.float32),
         "TriN": np.where(t[:, None] >= t[None, :], -1.0, 0.0).astype(np.float32),
         "OnesN": np.full((128, 128), -1.0, np.float32)}
    return c


def a_in_maps(inp, ntiles=SEQ // 128):
    w = inp["hy_w_in"][0]
    cuts = np.cumsum([0, 256, 256, 512, 512, 16, 512, 512, 512])
    o_gq, o_gk, o_gv, o_gr, o_glr, o_sq, o_sk, o_sv = cuts[:8]
    consts = a_consts()
    maps = []
    for c in range(NCORES):
        b, g = c // 4, c % 4
        cols = np.concatenate([
            np.arange(o_gq + 64 * g, o_gq + 64 * g + 64), np.arange(o_gk + 64 * g, o_gk + 64 * g + 64),
            np.arange(o_glr, o_glr + 16),
            np.arange(o_sq + 128 * g, o_sq + 128 * g + 128), np.arange(o_sk + 128 * g, o_sk + 128 * g + 128),
            np.arange(o_gv + 128 * g, o_gv + 128 * g + 128), np.arange(o_gr + 128 * g, o_gr + 128 * g + 128),
            np.arange(o_sv + 128 * g, o_sv + 128 * g + 128), np.arange(o_gk + 64 * g, o_gk + 64 * g + 64)])
        assert len(cols) == NWA
        m = {"x": inp["x"][b, :ntiles * 128], "w_in": w[:, cols], "w_gate2": inp["hy_w_gate2"][0][:, 64 * g:64 * g + 64],
             "b_gate": inp["hy_b_gate"][0][64 * g:64 * g + 64], "gla_norm": inp["hy_gla_norm"][0],
             "q_norm": inp["hy_sb_q_norm"][0], "k_norm": inp["hy_sb_k_norm"][0], "mix_norm0": inp["mix_norm"][0]}
        m.update(consts)
        maps.append({k: np.ascontiguousarray(v) for k, v in m.items()})
    return maps


def assemble_oT(res_a, T=SEQ):
    oT_full = np.zeros((2, D, T), res_a[0]["oT_out"].dtype)
    for c in range(NCORES):
        b, g = c // 4, c % 4
        o = res_a[c]["oT_out"]
        oT_full[b, 128 * g:128 * g + 128] = o[0:128]
        oT_full[b, 512 + 128 * g:512 + 128 * g + 128] = o[128:256]
    return oT_full


def build_fused():
    nc = bass.Bass("TRN2", target_bir_lowering=False)
    dr = {}
    for n, sh, dt in A_INPUTS + B_INPUTS + [("sel", [128, 4], F32)]:
        if n in dr or n == "oT":
            continue
        dr[n] = dram_in(nc, n, sh, dt)
    dr["out"] = nc.dram_tensor("out", [TOK, D], F32, kind="ExternalOutput").ap()
    ag_in = [nc.dram_tensor("ag_in%d" % i, [256, AGC], BF16) for i in range(NAGC)]
    ag_out = [nc.dram_tensor("ag_out%d" % i, [4 * 256, AGC], BF16) for i in range(NAGC)]
    P = Prog(nc)
    zst = P.scope()
    zt = P.sb("zt", [128, HALO], BF16, zst)
    P.op("pool", lambda e: e.memset(zt[:], 0.0), writes=[zt])
    ztoks = []
    for r in range(2):
        tl = Tile(None, "ag_zero%d" % r)
        P.dma("sp", ag_in[0].ap()[r * 128:(r + 1) * 128, 0:HALO], zt[:], reads=[zt], writes=[tl])
        ztoks.append((0, tl))
    dra = dict(dr)
    dra["ag_chunks"] = [t.ap() for t in ag_in]
    ag_state = {"next": 0, "tok": None}

    def ag_ready(ntok_done, outs_):
        while ag_state["next"] < NAGC and (ag_state["next"] + 1) * AGC <= HALO + ntok_done:
            ci = ag_state["next"]
            toks = [t.lw for (c_, t) in outs_ + ztoks if c_ == ci]
            ag_state["tok"] = P.collective_allgather(ag_in[ci].ap().opt(), ag_out[ci].ap().opt(), [[0, 1, 2, 3], [4, 5, 6, 7]], toks)
            ag_state["next"] += 1

    dra["ag_ready"] = ag_ready
    outs = phase_a(P, dra)
    ag_ready(SEQ, outs)
    assert ag_state["next"] == NAGC
    ag_tok = ag_state["tok"]
    P.barrier()
    for st_ in reversed(P.pools):
        st_.close()
    P.pools = []
    outd = phase_b(P, dr, ag=([t.ap() for t in ag_out], ag_tok))
    P.finish(outd)
    P.close()
    return nc


def w_out_perm(w_out):
    idx = np.concatenate([np.concatenate([np.arange(128 * g, 128 * g + 128), np.arange(512 + 128 * g, 512 + 128 * g + 128)])
                          for g in range(4)])
    return w_out[idx]


def kernel(**inp):
    inp = {k: np.asarray(v) for k, v in inp.items()}
    nc = build_fused()
    amaps = a_in_maps(inp)
    bmaps = b_in_maps(inp, None)
    maps = []
    for c in range(NCORES):
        m = dict(amaps[c])
        m.update(bmaps[c])
        m["w_out"] = np.ascontiguousarray(w_out_perm(inp["hy_w_out"][0]))
        sel = np.zeros((128, 4), np.float32)
        sel[:, c % 4] = 1.0
        m["sel"] = sel
        maps.append(m)
    res = run_bass_kernel_spmd(nc, maps, core_ids=list(range(NCORES)))
    out = np.concatenate([r["out"] for r in res.results], 0).reshape(2, SEQ, D)
    return out.astype(np.float32)
```

```python
import numpy as np
import ml_dtypes
from contextlib import ExitStack
import concourse.bass as bass
import concourse.mybir as mybir
from concourse.bass_utils import run_bass_kernel_spmd

F32 = mybir.dt.float32
BF16 = mybir.dt.bfloat16
AF = mybir.ActivationFunctionType
ALU = mybir.AluOpType

D = 1024
DFF = 2816
SEQ = 8192
NCORES = 8
TOK = 2048
HALO = 128
NTB = (TOK + HALO) // 128
EPS = 1e-6


class Tile:
    __slots__ = ("t", "name", "lw", "rd", "psum")

    def __init__(self, t, name, psum=False):
        self.t = t
        self.name = name
        self.lw = None
        self.rd = []
        self.psum = psum

    def __getitem__(self, idx):
        return self.t[idx]


class Prog:
    ENGS = ("pe", "act", "dve", "pool", "sp")

    def __init__(self, nc, ndma_sems=14):
        self.nc = nc
        self.es = ExitStack()
        self.eng = {"pe": nc.tensor, "act": nc.scalar, "dve": nc.vector,
                    "pool": nc.gpsimd, "sp": nc.sync}
        self.sem = {}
        self.cnt = {}
        for e in ("pe", "act", "dve", "pool"):
            self.sem[e] = self.es.enter_context(nc.semaphore("c_" + e))
            self.cnt[e] = 0
        self.dq = {}
        self.dsem = {}
        for q in ("sp", "pool"):
            sems = [self.es.enter_context(nc.semaphore("d_%s%d" % (q, i))) for i in range(ndma_sems)]
            self.dq[q] = {"n": ndma_sems, "val": [0] * ndma_sems, "next": 0}
            for i, s in enumerate(sems):
                self.dsem[(q, i)] = s
        self.seen = {e: {} for e in self.ENGS}
        self.ninst = 0
        self.nwaits = 0
        self.pools = []

    def _uid(self):
        self.uid = getattr(self, "uid", 0) + 1
        return self.uid

    def scope(self):
        st = ExitStack()
        self.pools.append(st)
        return st

    def sb(self, name, shape, dt, st=None):
        t = (st or self.es).enter_context(self.nc.sbuf_tensor("s%d_%s" % (self._uid(), name), list(shape), dt))
        return Tile(t, name)

    def ps(self, name, shape, dt=F32, st=None):
        t = (st or self.es).enter_context(self.nc.psum_tensor("p%d_%s" % (self._uid(), name), list(shape), dt))
        return Tile(t, name, psum=True)

    def _wait(self, e, tok):
        if tok is None:
            return
        if tok[0] == "e":
            key = ("e", tok[1])
            sem = self.sem[tok[1]]
            val = tok[2]
        else:
            key = ("d", tok[1])
            sem = self.dsem[tok[1]]
            val = tok[2]
        if self.seen[e].get(key, 0) >= val:
            return
        self.seen[e][key] = val
        self.eng[e].wait_ge(sem, val)
        self.nwaits += 1

    def _deps(self, e, reads, writes):
        for t in reads:
            if t is not None:
                self._wait(e, t.lw)
                if t.psum:
                    for r in t.rd:
                        if not (r[0] == "e" and r[1] == e):
                            self._wait(e, r)
        for t in writes:
            if t is None:
                continue
            lw = t.lw
            if not (e == "pe" and lw is not None and lw[0] == "e" and lw[1] == e):
                self._wait(e, lw)
            for r in t.rd:
                if e == "pe" and r[0] == "e" and r[1] == e:
                    continue
                self._wait(e, r)

    def _mark(self, tok, reads, writes):
        for t in reads:
            if t is not None:
                t.rd = [r for r in t.rd if not (r[0] == tok[0] and r[1] == tok[1])] + [tok]
        for t in writes:
            if t is not None:
                t.lw = tok
                t.rd = []

    def op(self, e, fn, reads=(), writes=()):
        self._deps(e, reads, writes)
        ins = fn(self.eng[e])
        self.cnt[e] += 1
        ins.then_inc(self.sem[e], 1)
        self._mark(("e", e, self.cnt[e]), reads, writes)
        self.ninst += 1
        return ins

    def dma(self, q, out, in_, reads=(), writes=(), **kw):
        d = self.dq[q]
        i = d["next"]
        d["next"] = (i + 1) % d["n"]
        skey = (q, i)
        if d["val"][i] > 0:
            self._wait(q, ("d", skey, d["val"][i]))
        self._deps(q, reads, writes)
        ins = self.eng[q].dma_start(out=out, in_=in_, **kw)
        d["val"][i] += 16
        ins.then_inc(self.dsem[skey], 16)
        tok = ("d", skey, d["val"][i])
        self._mark(tok, reads, writes)
        self.ninst += 1
        return tok

    def collective_allgather(self, in_ap, out_ap, groups, wait_toks):
        for tok in wait_toks:
            self._wait("pool", tok)
        if ("cc", 0) not in self.dsem:
            self.dsem[("cc", 0)] = self.es.enter_context(self.nc.semaphore("cc_sem"))
            self.ncc = 0
        ins = self.nc.gpsimd.collective_compute("AllGather", ALU.bypass, replica_groups=groups,
                                                ins=[in_ap], outs=[out_ap])
        self.ncc += 1
        ins.then_inc(self.dsem[("cc", 0)], 1)
        self.ninst += 1
        return ("d", ("cc", 0), self.ncc)

    def barrier(self):
        for e in self.ENGS:
            for y in ("pe", "act", "dve", "pool"):
                if y != e and self.cnt[y] > 0:
                    self._wait(e, ("e", y, self.cnt[y]))
            for q, d in self.dq.items():
                for i in range(d["n"]):
                    if d["val"][i] > 0:
                        self._wait(e, ("d", (q, i), d["val"][i]))

    def finish(self, out_tiles=()):
        for t in out_tiles:
            self._wait("sp", t.lw)

    def close(self):
        for st in reversed(self.pools):
            st.close()
        self.es.close()


def mm(P, out_t, out_ap, lhsT_t, lhsT_ap, rhs_t, rhs_ap, start, stop, skip=False):
    P.op("pe", lambda e: e.matmul(out_ap, lhsT=lhsT_ap, rhs=rhs_ap, start=start, stop=stop, skip_group_check=skip),
         reads=[lhsT_t, rhs_t], writes=[out_t])


def load_cols(P, q, dst, src_vec, nchunk):
    P.dma(q, dst[:], src_vec.rearrange("(k p) -> p k", p=128), writes=[dst],
          allow_slow_non_contiguous=True)


class Ctx:
    pass


def norm_T(P, C, src_t, src_ap, gcol, dst_t, dst_col0):
    i = C.nrm_i
    C.nrm_i += 1
    hs = C.hs[i % 2]
    pT = C.pT[i % 2]
    ss = C.ss[i % 2]
    P.op("act", lambda e: e.activation(out=C.junk[:], in_=src_ap, func=AF.Square, accum_out=ss[:, 0:1]),
         reads=[src_t], writes=[C.junk, ss])
    P.op("act", lambda e: e.activation(out=ss[:, 1:2], in_=ss[:, 0:1], func=AF.Sqrt, scale=1.0 / D, bias=C.epsc[:, 0:1]),
         reads=[ss, C.epsc], writes=[ss])
    P.op("dve", lambda e: e.reciprocal(out=ss[:, 2:3], in_=ss[:, 1:2]), reads=[ss], writes=[ss])
    P.op("dve", lambda e: e.tensor_scalar(out=hs[:], in0=src_ap, scalar1=ss[:, 2:3], scalar2=None, op0=ALU.mult),
         reads=[src_t, ss], writes=[hs])
    for k in range(8):
        P.op("pe", lambda e: e.transpose(out=pT[:, k, :], in_=hs[:, k * 128:(k + 1) * 128], identity=C.identb[:]),
             reads=[hs, C.identb], writes=[pT])
    for k in range(8):
        P.op("act", lambda e: e.activation(out=dst_t[:, k, dst_col0:dst_col0 + 128], in_=pT[:, k, :], func=AF.Copy,
                                           scale=gcol[:, k:k + 1]),
             reads=[pT, gcol], writes=[dst_t])


def set_gain(P, C, grow_d):
    ones_row = C.wdwc[0:1, 0:4, :].rearrange("p a b -> p (a b)")
    P.dma("sp", ones_row, C.ones_d[0:1, :], writes=[C.wdwc])
    grow = C.tmp[2]
    g2 = grow_d.rearrange("(o n) -> o n", o=1)
    for half in range(2):
        P.dma("sp", grow[0:1, :], g2[0:1, half * 512:(half + 1) * 512], writes=[grow])
        pd = C.pd[half]
        P.op("pe", lambda e: e.matmul(pd[:, :], lhsT=ones_row, rhs=grow[0:1, :], start=True, stop=True), reads=[C.wdwc, grow], writes=[pd])
        P.op("dve", lambda e: e.tensor_copy(out=C.gbc[:, half * 512:(half + 1) * 512], in_=pd[:, :]), reads=[pd], writes=[C.gbc])


def norm_T2_a(P, C, src_t, src_ap):
    i = C.nrm_i
    C.nrm_i += 1
    hs = C.hs[i % 2]
    ss = C.ss[i % 2]
    P.op("act", lambda e: e.activation(out=hs[:], in_=src_ap, func=AF.Square, accum_out=ss[:, 0:1]),
         reads=[src_t], writes=[hs, ss])
    P.op("act", lambda e: e.activation(out=ss[:, 1:2], in_=ss[:, 0:1], func=AF.Sqrt, scale=1.0 / D, bias=C.epsc[:, 0:1]),
         reads=[ss, C.epsc], writes=[ss])
    P.op("dve", lambda e: e.reciprocal(out=ss[:, 2:3], in_=ss[:, 1:2]), reads=[ss], writes=[ss])
    P.op("dve", lambda e: e.scalar_tensor_tensor(out=hs[:], in0=src_ap, scalar=ss[:, 2:3], in1=C.gbc[:], op0=ALU.mult, op1=ALU.mult),
         reads=[src_t, ss, C.gbc], writes=[hs])
    return i


def norm_T2_b(P, C, i, dst_view, dst_tensor, dst_col0):
    hs = C.hs[i % 2]
    pT = C.pT[i % 2]
    pTb = pT[:].bitcast(BF16)
    for k in range(8):
        P.op("pe", lambda e: e.transpose(out=pTb[:, k * 128:(k + 1) * 128], in_=hs[:, k * 128:(k + 1) * 128], identity=C.identb[:]),
             reads=[hs, C.identb], writes=[pT])
    P.op("act", lambda e: e.copy(out=dst_tensor[:, :, dst_col0:dst_col0 + 128], in_=pTb.rearrange("p (k t) -> p k t", k=8)),
         reads=[pT], writes=[dst_view])


def norm_T2(P, C, src_t, src_ap, dst_view, dst_tensor, dst_col0):
    i = norm_T2_a(P, C, src_t, src_ap)
    norm_T2_b(P, C, i, dst_view, dst_tensor, dst_col0)


def tg_of_tile(i):
    return 0 if i == 0 else 1 + (i - 1) // 4


def norm_TA_a(P, C, src_t, src_ap):
    i = C.nrm_i
    C.nrm_i += 1
    hs = C.hs[i % 2]
    ss = C.ss[i % 2]
    P.op("act", lambda e: e.activation(out=hs[:], in_=src_ap, func=AF.Square, accum_out=ss[:, 0:1]),
         reads=[src_t], writes=[hs, ss])
    P.op("act", lambda e: e.activation(out=ss[:, 1:2], in_=ss[:, 0:1], func=AF.Ln, scale=1.0 / D, bias=C.epsc[:, 0:1]),
         reads=[ss, C.epsc], writes=[ss])
    P.op("act", lambda e: e.activation(out=ss[:, 2:3], in_=ss[:, 1:2], func=AF.Exp, scale=-0.5), reads=[ss], writes=[ss])
    P.op("dve", lambda e: e.scalar_tensor_tensor(out=hs[:], in0=src_ap, scalar=ss[:, 2:3], in1=C.gbc[:], op0=ALU.mult, op1=ALU.mult),
         reads=[src_t, ss, C.gbc], writes=[hs])
    return i


def norm_TA_b(P, C, i, dst_t, dst_col0):
    hs = C.hs[i % 2]
    pT = C.pT[i % 2]
    for k in range(8):
        P.op("pe", lambda e: e.transpose(out=pT[:, k, :], in_=hs[:, k * 128:(k + 1) * 128], identity=C.identb[:]),
             reads=[hs, C.identb], writes=[pT])
    P.op("act", lambda e: e.copy(out=dst_t[:, :, dst_col0:dst_col0 + 128], in_=pT[:, :, :]), reads=[pT], writes=[dst_t])


def glu_stage(P, C, hnT, tgs, ngroups, load_w, epilogue, down, slot0=0):
    load_w(0, slot0 % 2)
    for g in range(ngroups):
        slot = (g + slot0) % 2
        if g + 1 < ngroups:
            load_w(g + 1, (g + 1 + slot0) % 2)
        hT, hviews = hnT
        for (t0, n) in tgs:
            hvw = hviews[tg_of_tile(t0 // 128)]
            for c in range(2):
                pb = C.gu_i % 2
                C.gu_i += 1
                pg, pu = C.pg[pb], C.pu[pb]
                for k in range(8):
                    mm(P, pg, pg[:, 0:n], C.wg[slot], C.wg[slot][:, k, c * 128:(c + 1) * 128],
                       hvw, hT[:, k, t0:t0 + n], k == 0, k == 7)
                for k in range(8):
                    mm(P, pu, pu[:, 0:n], C.wu[slot], C.wu[slot][:, k, c * 128:(c + 1) * 128],
                       hvw, hT[:, k, t0:t0 + n], k == 0, k == 7)
                epilogue(g, c, t0, n, pg, pu)
            if down is not None:
                down(g, slot, t0, n)


def ffn_layer(P, C, h, hv, hnT, tgs, wg_d, wu_d, wd_d, before_last=None, after_last_down=None, slot0=0):
    NGF = DFF // 256

    def load_w(g, slot):
        f0 = g * 256
        P.dma("pool", C.wg[slot][:], wg_d[:, f0:f0 + 256].rearrange("(k p) n -> p k n", p=128), writes=[C.wg[slot]])
        P.dma("pool", C.wu[slot][:], wu_d[:, f0:f0 + 256].rearrange("(k p) n -> p k n", p=128), writes=[C.wu[slot]])
        P.dma("pool", C.wd[slot][:], wd_d[f0:f0 + 256, :].rearrange("(c p) n -> p c n", p=128), writes=[C.wd[slot]])

    def epilogue(g, c, t0, n, pg, pu):
        sg = C.tmp[C.tmp_i % 2]
        C.tmp_i += 1
        at = C.actT[C.act_i % 2]
        P.op("act", lambda e: e.activation(out=sg[:, 0:n], in_=pg[:, 0:n], func=AF.Silu), reads=[pg], writes=[sg])
        P.op("dve", lambda e: e.tensor_tensor(out=at[:, c, 0:n], in0=sg[:, 0:n], in1=pu[:, 0:n], op=ALU.mult),
             reads=[sg, pu], writes=[at])

    def down(g, slot, t0, n):
        at = C.actT[C.act_i % 2]
        C.act_i += 1
        if g == NGF - 1 and t0 == tgs[0][0] and before_last is not None:
            before_last()
        for j in range(n // 128):
            ti = (t0 + j * 128) // 128
            for half in range(2):
                pd = C.pd4[C.pd_i % 4]
                C.pd_i += 1
                for c in range(2):
                    mm(P, pd, pd[:, :], at, at[:, c, j * 128:(j + 1) * 128],
                       C.wd[slot], C.wd[slot][:, c, half * 512:(half + 1) * 512], c == 0, c == 1)
                P.op("dve", lambda e: e.tensor_tensor(out=h[:, ti, half * 512:(half + 1) * 512],
                                                      in0=h[:, ti, half * 512:(half + 1) * 512], in1=pd[:, :], op=ALU.add),
                     reads=[hv[ti], pd], writes=[hv[ti]])
        if g == NGF - 1 and after_last_down is not None:
            after_last_down(t0, n)

    glu_stage(P, C, hnT, tgs, NGF, load_w, epilogue, down, slot0)


def proj_accum(P, C, h, hv, tiles, src, src_col0, wres, bias_row, after_tile=None):
    srcT_tensor, src_views = src
    for idx, ti in enumerate(tiles):
        c0 = src_col0 + idx * 128
        srcT = srcT_tensor
        sv = src_views(c0) if callable(src_views) else src_views
        for half in range(2):
            pd = C.pd[half]
            for k in range(8):
                P.op("pe", lambda e: e.matmul(pd[:, :], lhsT=srcT[:, k, c0:c0 + 128], rhs=wres[:, k, half * 512:(half + 1) * 512],
                                              start=(k == 0), stop=((k == 7) and bias_row is None)),
                     reads=list(sv) + [wres], writes=[pd])
            if bias_row is not None:
                mm(P, pd, pd[:, :], C.onesb, C.onesb[0:1, :], bias_row, bias_row[0:1, half * 512:(half + 1) * 512],
                   False, True)
            P.op("dve", lambda e: e.tensor_tensor(out=h[:, ti, half * 512:(half + 1) * 512],
                                                  in0=h[:, ti, half * 512:(half + 1) * 512], in1=pd[:, :], op=ALU.add),
                 reads=[hv[ti], pd], writes=[hv[ti]])
        if after_tile is not None:
            after_tile(ti)


def phase_b(P, dr, ag=None):
    st = P.scope()
    C = Ctx()
    C.nrm_i = C.gu_i = C.tmp_i = C.act_i = 0
    h = P.sb("h", [128, NTB, D], F32, st)
    hv = [Tile(h.t, "h%d" % i) for i in range(NTB)]
    hnT = P.sb("hnT", [128, 8, NTB * 128], BF16, st)
    oT = P.sb("oTsb", [128, 8, NTB * 128], BF16, st)
    wres = P.sb("wres", [128, 8, D], BF16, st)
    C.wg = [P.sb("wg%d" % i, [128, 8, 256], BF16, st) for i in range(2)]
    C.wu = [P.sb("wu%d" % i, [128, 8, 256], BF16, st) for i in range(2)]
    C.wd = [P.sb("wd%d" % i, [128, 2, D], BF16, st) for i in range(2)]
    C.tmp = [P.sb("tmp%d" % i, [128, 512], F32, st) for i in range(3)]
    C.actT = [P.sb("actT%d" % i, [128, 2, 512], BF16, st) for i in range(2)]
    C.hs = [P.sb("hs%d" % i, [128, D], BF16, st) for i in range(2)]
    C.ss = [P.sb("ss%d" % i, [128, 4], F32, st) for i in range(2)]
    C.gbc = P.sb("gbc", [128, D], F32, st)
    C.ones_d = dr["ones"]
    C.identb = P.sb("identb", [128, 128], BF16, st)
    C.onesb = P.sb("onesb", [128, 128], BF16, st)
    C.epsc = P.sb("epsc", [128, 1], F32, st)
    b1col = P.sb("b1col", [128, 16], F32, st)
    bdwcol = P.sb("bdwcol", [128, 8], F32, st)
    lngcol = P.sb("lngcol", [128, 8], F32, st)
    lnbcol = P.sb("lnbcol", [128, 8], F32, st)
    b2row = P.sb("b2row", [1, D], BF16, st)
    hmask = P.sb("hmask", [128, 1], F32, st)
    wdw = P.sb("wdw", [31, D], BF16, st)
    wdwc = P.sb("wdwc", [128, 8, 32], F32, st)
    C.wdwc = wdwc
    dg = P.sb("dg", [128, 31, 128], BF16, st)
    C.pT = [P.ps("pT%d" % i, [128, 512], F32, st) for i in range(2)]
    C.pg = [P.ps("pg%d" % i, [128, 512], F32, st) for i in range(2)]
    C.pu = [P.ps("pu%d" % i, [128, 512], F32, st) for i in range(2)]
    C.pd = [P.ps("pd%d" % i, [128, 512], F32, st) for i in range(2)]
    C.pd4 = C.pd + C.pT
    C.pd_i = 0
    outd = [Tile(dr["out"], "out%d" % i) for i in range(NTB)]

    P.dma("pool", C.identb[:], dr["ident"][:, :], writes=[C.identb])
    P.dma("pool", C.onesb[:], dr["ones"][:, :], writes=[C.onesb])
    P.dma("sp", C.epsc[:], dr["epsc"][:, :], writes=[C.epsc])
    P.dma("sp", hmask[:], dr["hmask"][:, :], writes=[hmask])
    load_cols(P, "sp", b1col, dr["b_pw1"], 16)
    load_cols(P, "sp", bdwcol, dr["b_dw"], 8)
    load_cols(P, "sp", lngcol, dr["ln_g"], 8)
    load_cols(P, "sp", lnbcol, dr["ln_b"], 8)
    P.dma("pool", b2row[:], dr["b_pw2"].rearrange("(o n) -> o n", o=1), writes=[b2row])
    P.dma("pool", wdw[:], dr["w_dw"][:, :], writes=[wdw])
    P.dma("pool", wres[:], dr["w_out"].rearrange("(k p) n -> p k n", p=128), writes=[wres])
    for i in range(NTB):
        P.dma("sp", h[:, i, :], dr["xh"][i * 128:(i + 1) * 128, :], writes=[hv[i]])
    if ag is None:
        for k in range(8):
            P.dma("sp", oT[:, k, :], dr["oT"][k * 128:(k + 1) * 128, :], writes=[oT])
    else:
        ag_chunks, ag_tok = ag
        sel = P.sb("sel", [128, 4], F32, st)
        P.dma("sp", sel[:], dr["sel"][:, :], writes=[sel])
        dsel = Tile(dg.t, "dsel")
        for j in range(4):
            P.op("dve", lambda e: e.tensor_scalar(out=dg[:, j, :], in0=C.identb[:], scalar1=sel[:, j:j + 1], scalar2=None, op0=ALU.mult),
                 reads=[C.identb, sel], writes=[dsel])
        P._wait("sp", ag_tok)
        P._wait("pool", ag_tok)
        cst = [Tile(hnT.t, "cst%d" % i) for i in range(8)]
        banks = [C.pg[0], C.pg[1], C.pu[0], C.pu[1]]
        nb = 0
        for k in range(8):
            for j in range(4):
                cs = cst[(k % 2) * 4 + j]
                for (ci, off, sw, so) in pad_segments(j * TOK, NTB * 128):
                    P.dma("sp" if j % 2 == 0 else "pool", hnT[:, (k % 2) * 4 + j, so:so + sw],
                          ag_chunks[ci][k * 128:(k + 1) * 128, off:off + sw], writes=[cs])
            for (t0, n) in [(0, 128)] + [(128 + 512 * i, 512) for i in range(4)]:
                pb = banks[nb % 4]
                for j in range(4):
                    cs = cst[(k % 2) * 4 + j]
                    P.op("pe", lambda e: e.matmul(pb[:, 0:n], lhsT=dg[:, j, :], rhs=hnT[:, (k % 2) * 4 + j, t0:t0 + n], start=(j == 0), stop=(j == 3)),
                         reads=[dsel, cs], writes=[pb])
                if nb % 2 == 0:
                    P.op("act", lambda e: e.copy(out=oT[:, k, t0:t0 + n], in_=pb[:, 0:n]), reads=[pb], writes=[oT])
                else:
                    P.op("dve", lambda e: e.tensor_copy(out=oT[:, k, t0:t0 + n], in_=pb[:, 0:n]), reads=[pb], writes=[oT])
                nb += 1
        P.barrier()

    all_tiles = list(range(NTB))
    own_tiles = list(range(1, NTB))
    tgs_all = [(0, 128)] + [(128 + 512 * i, 512) for i in range(4)]
    tgs_own = [(128 + 512 * i, 512) for i in range(4)]

    hn_v = [Tile(hnT.t, "hn_v%d" % i) for i in range(5)]
    oT_v = [Tile(oT.t, "oT_v%d" % i) for i in range(5)]
    P.barrier()

    def norm_tiles(tiles_, dst, dst_views):
        for i in tiles_:
            norm_T2(P, C, hv[i], h[:, i, :], dst_views[tg_of_tile(i)], dst, i * 128)

    class Lagged:
        def __init__(self, dst, dst_views, lag):
            self.q1, self.q2, self.dst, self.dv, self.lag = [], [], dst, dst_views, lag

        def _s1(self):
            tiles_ = self.q1.pop(0)
            self.q2.append([(i, norm_T2_a(P, C, hv[i], h[:, i, :])) for i in tiles_])

        def _s2(self):
            for (i, idx) in self.q2.pop(0):
                norm_T2_b(P, C, idx, self.dv[tg_of_tile(i)], self.dst, i * 128)

        def push(self, tiles_):
            for i in tiles_:
                self.q1.append([i])
                if self.q2:
                    self._s2()
                if len(self.q1) > self.lag:
                    self._s1()

        def flush(self):
            while self.q1 or self.q2:
                if self.q2:
                    self._s2()
                if self.q1:
                    self._s1()

    set_gain(P, C, dr["ffn_norm0"])
    lg0 = Lagged(hnT, hn_v, 1)
    proj_accum(P, C, h, hv, all_tiles, (oT, lambda c0: [oT_v[tg_of_tile(c0 // 128)]]), 0, wres, None,
               after_tile=lambda ti: lg0.push([ti]))
    lg0.flush()

    def conv_gain():
        set_gain(P, C, dr["mix_norm1"])

    lg1 = Lagged(hnT, hn_v, 1)

    def conv_norm(t0, n):
        lg1.push(range(t0 // 128, (t0 + n) // 128))

    ffn_layer(P, C, h, hv, (hnT, hn_v), tgs_all, dr["fg0"], dr["fu0"], dr["fd0"], before_last=conv_gain, after_last_down=conv_norm)
    lg1.flush()

    P.dma("pool", wres[:], dr["w_pw2"].rearrange("(k p) n -> p k n", p=128), writes=[wres])
    uT = oT

    def load_w1(g, slot):
        f0 = g * 256
        P.dma("pool", C.wu[slot][:], dr["w_pw1"][:, f0:f0 + 256].rearrange("(k p) n -> p k n", p=128), writes=[C.wu[slot]])
        P.dma("pool", C.wg[slot][:], dr["w_pw1"][:, D + f0:D + f0 + 256].rearrange("(k p) n -> p k n", p=128), writes=[C.wg[slot]])

    def epi1(g, c, t0, n, pg, pu):
        ch = g * 2 + c
        uv = oT_v[tg_of_tile(t0 // 128)]
        sg = C.tmp[C.tmp_i % 2]
        C.tmp_i += 1
        P.op("act", lambda e: e.activation(out=sg[:, 0:n], in_=pg[:, 0:n], func=AF.Sigmoid, bias=b1col[:, 8 + ch:9 + ch]),
             reads=[pg, b1col], writes=[sg])
        P.op("dve", lambda e: e.scalar_tensor_tensor(out=uT[:, ch, t0:t0 + n], in0=pu[:, 0:n], scalar=b1col[:, ch:ch + 1],
                                                     in1=sg[:, 0:n], op0=ALU.add, op1=ALU.mult),
             reads=[pu, b1col, sg], writes=[uv])
        if t0 == 0:
            P.op("dve", lambda e: e.tensor_scalar(out=uT[:, ch, 0:128], in0=uT[:, ch, 0:128], scalar1=hmask[:, 0:1], scalar2=None,
                                                  op0=ALU.mult), reads=[uv, hmask], writes=[uv])

    glu_stage(P, C, (hnT, hn_v), tgs_all, 4, load_w1, epi1, None, slot0=1)
    for ch in range(8):
        pw = C.pT[ch % 2]
        pwb = pw[:].bitcast(BF16)
        P.op("pe", lambda e: e.transpose(out=pwb[:, 0:31], in_=wdw[0:31, ch * 128:(ch + 1) * 128], identity=C.identb[0:31, 0:31]),
             reads=[wdw, C.identb], writes=[pw])
        P.op("dve", lambda e: e.tensor_copy(out=wdwc[:, ch, 0:31], in_=pwb[:, 0:31]), reads=[pw], writes=[wdwc])
    cvT = hnT
    dgt = [Tile(dg.t, "dg%d" % j) for j in range(31)]
    pcs = [C.pg[0], C.pg[1], C.pu[0], C.pu[1]]
    for ch in range(8):
        for j in range(31):
            if j % 3 == 2:
                P.op("act", lambda e: e.activation(out=dg[:, j, :], in_=C.identb[:], func=AF.Copy, scale=wdwc[:, ch, j:j + 1]),
                     reads=[C.identb, wdwc], writes=[dgt[j]])
            else:
                P.op("dve", lambda e: e.tensor_scalar(out=dg[:, j, :], in0=C.identb[:], scalar1=wdwc[:, ch, j:j + 1], scalar2=None,
                                                      op0=ALU.mult), reads=[C.identb, wdwc], writes=[dgt[j]])
            for tg in range(4):
                s0 = 128 + 512 * tg - 30 + j
                P.op("pe", lambda e: e.matmul(pcs[tg][:, :], lhsT=dg[:, j, :], rhs=uT[:, ch, s0:s0 + 512], start=(j == 0), stop=(j == 30)),
                     reads=[dgt[j], oT_v[tg], oT_v[tg + 1]], writes=[pcs[tg]])
        for tg in range(4):
            pc = pcs[tg]
            P.op("act", lambda e: e.activation(out=cvT[:, ch, 512 * tg:512 * tg + 512], in_=pc[:, :], func=AF.Identity,
                                               bias=bdwcol[:, ch:ch + 1]), reads=[pc, bdwcol], writes=hn_v)
    dgf = dg[:].rearrange("p a b -> p (a b)").bitcast(F32)
    ln_mean = [(C.tmp[0], C.tmp[0][:]), (Tile(None, "lnm1"), dgf[:, 0:512])]
    ln_rstd = [(C.tmp[1], C.tmp[1][:]), (Tile(None, "lnr1"), dgf[:, 512:1024])]
    ln_msq = (Tile(None, "lnq"), dgf[:, 1024:1536])
    ln_banks = [(C.pu[0], C.pu[1]), (C.pd[0], C.pd[1])]
    ln_t1 = [(C.tmp[2], C.tmp[2][:]), (C.hs[0], C.hs[0][:].bitcast(F32)), (C.hs[1], C.hs[1][:].bitcast(F32))]

    def ln_stats(tg):
        c0 = 512 * tg
        s1, s2 = ln_banks[tg % 2]
        (mean_t, mean), (rstd_t, rstd) = ln_mean[tg % 2], ln_rstd[tg % 2]
        msq_t, msq = ln_msq
        for ch in range(8):
            P.op("pe", lambda e: e.matmul(s1[:, :], lhsT=C.onesb[:, :], rhs=cvT[:, ch, c0:c0 + 512], start=(ch == 0), stop=(ch == 7)),
                 reads=[C.onesb] + hn_v, writes=[s1])
        for ch in range(8):
            sq = C.actT[ch % 2]
            P.op("act", lambda e: e.activation(out=sq[:, 0, :], in_=cvT[:, ch, c0:c0 + 512], func=AF.Square), reads=hn_v, writes=[sq])
            mm(P, s2, s2[:, :], C.onesb, C.onesb[:, :], sq, sq[:, 0, :], ch == 0, ch == 7)
        P.op("act", lambda e: e.activation(out=mean, in_=s1[:, :], func=AF.Copy, scale=1.0 / D), reads=[s1], writes=[mean_t])
        P.op("dve", lambda e: e.tensor_tensor(out=msq, in0=mean, in1=mean, op=ALU.mult), reads=[mean_t], writes=[msq_t])
        P.op("dve", lambda e: e.scalar_tensor_tensor(out=rstd, in0=s2[:, :], scalar=1.0 / D, in1=msq, op0=ALU.mult, op1=ALU.subtract),
             reads=[s2, msq_t], writes=[rstd_t])
        P.op("act", lambda e: e.activation(out=rstd, in_=rstd, func=AF.Sqrt, bias=C.epsc[:, 0:1]), reads=[rstd_t, C.epsc], writes=[rstd_t])
        P.op("dve", lambda e: e.reciprocal(out=rstd, in_=rstd), reads=[rstd_t], writes=[rstd_t])

    def ln_apply(tg):
        c0 = 512 * tg
        (mean_t, mean), (rstd_t, rstd) = ln_mean[tg % 2], ln_rstd[tg % 2]
        for ch in range(8):
            t1_t, t1 = ln_t1[ch % 3]
            P.op("dve", lambda e: e.tensor_tensor(out=t1, in0=cvT[:, ch, c0:c0 + 512], in1=mean, op=ALU.subtract),
                 reads=hn_v + [mean_t], writes=[t1_t])
            P.op("dve", lambda e: e.tensor_tensor(out=t1, in0=t1, in1=rstd, op=ALU.mult), reads=[t1_t, rstd_t], writes=[t1_t])
            P.op("act", lambda e: e.activation(out=cvT[:, ch, c0:c0 + 512], in_=t1, func=AF.Silu, scale=lngcol[:, ch:ch + 1],
                                               bias=lnbcol[:, ch:ch + 1]), reads=[t1_t, lngcol, lnbcol], writes=hn_v)

    ln_stats(0)
    for tg in range(4):
        if tg + 1 < 4:
            ln_stats(tg + 1)
        ln_apply(tg)
    set_gain(P, C, dr["ffn_norm1"])
    lg2 = Lagged(oT, oT_v, 1)
    proj_accum(P, C, h, hv, own_tiles, (cvT, hn_v), 0, wres, b2row, after_tile=lambda ti: lg2.push([ti]))
    lg2.flush()
    ffn_layer(P, C, h, hv, (oT, oT_v), tgs_own, dr["fg1"], dr["fu1"], dr["fd1"], slot0=1)
    for i in own_tiles:
        P.dma("sp", dr["out"][(i - 1) * 128:i * 128, :], h[:, i, :], reads=[hv[i]], writes=[outd[i]])
    return outd[1:]


AGC = 1664
NAGC = (HALO + SEQ) // AGC


def pad_segments(pc0, w):
    segs = []
    c = pc0
    while c < pc0 + w:
        ci, off = c // AGC, c % AGC
        sw = min(AGC - off, pc0 + w - c)
        segs.append((ci, off, sw, c - pc0))
        c += sw
    return segs


def dram_in(nc, name, shape, dt=F32):
    return nc.dram_tensor(name, list(shape), dt, kind="ExternalInput").ap()


B_INPUTS = [("xh", [TOK + HALO, D], F32), ("oT", [D, TOK + HALO], BF16), ("w_out", [D, D], F32),
            ("fg0", [D, DFF], F32), ("fu0", [D, DFF], F32), ("fd0", [DFF, D], F32),
            ("fg1", [D, DFF], F32), ("fu1", [D, DFF], F32), ("fd1", [DFF, D], F32),
            ("ffn_norm0", [D], F32), ("ffn_norm1", [D], F32), ("mix_norm1", [D], F32),
            ("w_pw1", [D, 2 * D], F32), ("b_pw1", [2 * D], F32), ("w_dw", [31, D], F32), ("b_dw", [D], F32),
            ("ln_g", [D], F32), ("ln_b", [D], F32), ("w_pw2", [D, D], F32), ("b_pw2", [D], F32),
            ("hmask", [128, 1], F32), ("ident", [128, 128], F32), ("ones", [128, 128], F32), ("epsc", [128, 1], F32)]


def build_b():
    nc = bass.Bass("TRN2", target_bir_lowering=False)
    dr = {n: dram_in(nc, n, s, dt) for n, s, dt in B_INPUTS}
    dr["out"] = nc.dram_tensor("out", [TOK, D], F32, kind="ExternalOutput").ap()
    P = Prog(nc)
    outd = phase_b(P, dr)
    P.finish(outd)
    P.close()
    return nc


def b_in_maps(inp, oT_full):
    x = inp["x"]
    maps = []
    for c in range(NCORES):
        b, j = c // 4, c % 4
        t0 = j * TOK
        xh = np.zeros((TOK + HALO, D), np.float32)
        xh[HALO:] = x[b, t0:t0 + TOK]
        if j > 0:
            xh[:HALO] = x[b, t0 - HALO:t0]
        oT = None
        if oT_full is not None:
            oT = np.zeros((D, TOK + HALO), oT_full.dtype)
            oT[:, HALO:] = oT_full[b][:, t0:t0 + TOK]
            if j > 0:
                oT[:, :HALO] = oT_full[b][:, t0 - HALO:t0]
        m = {"xh": xh, "oT": oT, "w_out": inp["hy_w_out"][0],
             "fg0": inp["ffn_w_gate"][0], "fu0": inp["ffn_w_up"][0], "fd0": inp["ffn_w_down"][0],
             "fg1": inp["ffn_w_gate"][1], "fu1": inp["ffn_w_up"][1], "fd1": inp["ffn_w_down"][1],
             "ffn_norm0": inp["ffn_norm"][0], "ffn_norm1": inp["ffn_norm"][1], "mix_norm1": inp["mix_norm"][1],
             "w_pw1": inp["cv_w_pw1"][0], "b_pw1": inp["cv_b_pw1"][0], "w_dw": inp["cv_w_dw"][0], "b_dw": inp["cv_b_dw"][0],
             "ln_g": inp["cv_ln_g"][0], "ln_b": inp["cv_ln_b"][0], "w_pw2": inp["cv_w_pw2"][0], "b_pw2": inp["cv_b_pw2"][0],
             "hmask": np.full((128, 1), 1.0 if j > 0 else 0.0, np.float32),
             "ident": np.eye(128, dtype=np.float32), "ones": np.ones((128, 128), np.float32),
             "epsc": np.full((128, 1), EPS, np.float32)}
        maps.append({k: np.ascontiguousarray(v) for k, v in m.items() if v is not None})
    return maps


NWA = 848


DBG = 99
SCHED = [1, 1, 2]


def phase_a(P, dr, ntiles=SEQ // 128):
    st = P.scope()
    C = Ctx()
    C.nrm_i = 0
    NT = ntiles
    NG = NT // 4
    T = NT * 128
    C.identb = P.sb("identb", [128, 128], BF16, st)
    C.onesb = P.sb("onesb", [128, 128], BF16, st)
    onesf = P.sb("onesf", [128, 128], F32, st)
    C.epsc = P.sb("epsc", [128, 1], F32, st)
    Dm = P.sb("Dm", [128, 128], F32, st)
    Bm = P.sb("Bm", [128, 128], F32, st)
    BD = P.sb("BD", [128, 128], F32, st)
    Ms = P.sb("Ms", [128, 128], BF16, st)
    TriN = P.sb("TriN", [128, 128], BF16, st)
    OnesN = P.sb("OnesN", [128, 128], BF16, st)
    for t_, n_, q_ in ((C.identb, "ident", "pool"), (C.onesb, "ones", "pool"), (onesf, "ones", "sp"), (C.epsc, "epsc", "sp"),
                       (Dm, "Dm", "sp"), (Bm, "Bm", "sp"), (BD, "BD", "sp"), (Ms, "Ms", "pool"), (TriN, "TriN", "pool"),
                       (OnesN, "OnesN", "pool")):
        P.dma(q_, t_[:], dr[n_][:, :], writes=[t_])
    w_sb = P.sb("w_sb", [128, 8, NWA], BF16, st)
    P.dma("pool", w_sb[:], dr["w_in"].rearrange("(k p) n -> p k n", p=128), writes=[w_sb])
    wg2 = P.sb("wg2", [16, 64], BF16, st)
    P.dma("pool", wg2[:], dr["w_gate2"][:, :], writes=[wg2])
    bgrow = P.sb("bgrow", [1, 64], BF16, st)
    P.dma("pool", bgrow[:], dr["b_gate"].rearrange("(o n) -> o n", o=1), writes=[bgrow])
    gnrow = P.sb("gnrow", [1, 128], F32, st)
    P.dma("sp", gnrow[:], dr["gla_norm"].rearrange("(o n) -> o n", o=1), writes=[gnrow])
    gqc = P.sb("gqc", [128, 2], F32, st)
    gkc = P.sb("gkc", [128, 1], F32, st)
    for hh_ in range(2):
        P.dma("sp", gqc[64 * hh_:64 * hh_ + 64, 0:1], dr["q_norm"].rearrange("(p o) -> p o", o=1), writes=[gqc])
        P.dma("sp", gkc[64 * hh_:64 * hh_ + 64, 0:1], dr["k_norm"].rearrange("(p o) -> p o", o=1), writes=[gkc])
    P.op("act", lambda e: e.mul(out=gqc[:, 1:2], in_=gqc[:, 0:1], mul=0.125), reads=[gqc], writes=[gqc])
    gcol0 = P.sb("gcol0", [128, 8], F32, st)
    load_cols(P, "sp", gcol0, dr["mix_norm0"], 8)
    qnT2 = P.sb("qnT2", [128, T], BF16, st)
    knT2 = P.sb("knT2", [128, T], BF16, st)
    BDb = P.sb("BDb", [128, 128], BF16, st)
    P.dma("pool", BDb[:], dr["BD"][:, :], writes=[BDb])
    vsb = P.sb("vsb", [128, NT, 128], BF16, st)
    gstage = [P.sb("gstage%d" % i, [128, 512], BF16, st) for i in range(2)]
    ostage = [P.sb("ostage%d" % i, [64, 512], BF16, st) for i in range(2)]
    outs = []
    xst = [P.sb("xst%d" % i, [128, D], F32, st) for i in range(2)]
    C.hs = [P.sb("hs%d" % i, [128, D], BF16, st) for i in range(2)]
    C.ss = [P.sb("ss%d" % i, [128, 4], F32, st) for i in range(2)]
    C.gbc = P.sb("gbcA", [128, D], F32, st)
    hnTg = [P.sb("hnTg%d" % i, [128, 8, 512], BF16, st) for i in range(2)]
    gqT = [P.sb("gqT%d" % i, [64, 512], BF16, st) for i in range(2)]
    gkT = [P.sb("gkT%d" % i, [64, 512], F32, st) for i in range(2)]
    glrT = [P.sb("glrT%d" % i, [16, 512], BF16, st) for i in range(2)]
    sqv = [P.sb("sqv%d" % i, [128, 512], BF16, st) for i in range(2)]
    sdt = [P.sb("sdt%d" % i, [128, 512], F32, st) for i in range(2)]
    gno = P.sb("gno", [128, 128], F32, st)
    BD4 = P.sb("BD4", [128, 4, 128], F32, st)
    vg = [P.sb("vg%d" % i, [128, 4, 128], BF16, st) for i in range(2)]
    sr = [P.sb("sr%d" % i, [128, 4, 128], F32, st) for i in range(2)]
    gktm = [P.sb("gktm%d" % i, [128, 4, 64], F32, st) for i in range(2)]
    e1 = P.sb("e1", [128, 4, 64], F32, st)
    lt = P.sb("lt", [128, 4, 64], F32, st)
    kdT = P.sb("kdT", [64, 4, 128], F32, st)
    qdT = P.sb("qdT", [64, 4, 128], F32, st)
    kdt = P.sb("kdt", [128, 4, 64], F32, st)
    kendT = P.sb("kendT", [64, 4, 128], BF16, st)
    qe0 = P.sb("qe0", [64, 4, 128], BF16, st)
    qe1 = P.sb("qe1", [64, 4, 128], BF16, st)
    ke0 = P.sb("ke0", [128, 4, 64], BF16, st)
    ke1 = P.sb("ke1", [128, 4, 64], BF16, st)
    scT = P.sb("scT", [128, 4, 128], BF16, st)
    NSR = 10
    Sf = [P.sb("Sf%d" % i, [64, 128], F32, st) for i in range(NSR)]
    Sb = [P.sb("Sb%d" % i, [64, 128], BF16, st) for i in range(NSR)]
    og = P.sb("og", [128, 4, 128], F32, st)
    ofin = P.sb("ofin", [128, 4, 128], BF16, st)
    ssg = P.sb("ssg", [128, 16], F32, st)
    Ef = [P.sb("Ef%d" % i, [128, 2, 512], F32, st) for i in range(2)]
    Lb = [P.sb("Lb%d" % i, [128, 2, 512], BF16, st) for i in range(3)]
    Wb = [P.sb("Wb%d" % i, [128, 2, 512], BF16, st) for i in range(2)]
    Rf = P.sb("Rf", [128, 2, 512], F32, st)
    Rb = [P.sb("Rb%d" % i, [128, 2, 512], BF16, st) for i in range(4)]
    pst = ExitStack()
    bank = [P.ps("bank%d" % i, [128, 512], F32, pst) for i in range(6)]
    C.pT = [P.ps("pT%d" % i, [128, 8, 128], BF16, pst) for i in range(2)]
    F0, F1, F2, GA, GB, GC = bank
    fa = F0

    for t_ in (qe0, qe1, ke0, ke1):
        P.op("pool", lambda e: e.memset(t_[:], 0.0), writes=[t_])
    P.op("pool", lambda e: e.memset(Sf[0][:], 0.0), writes=[Sf[0]])
    P.op("pool", lambda e: e.memset(Sb[0][:], 0.0), writes=[Sb[0]])
    for jt_ in range(4):
        P.dma("sp", BD4[:, jt_, :], dr["BD"][:, :], writes=[BD4])
    mm(P, fa, fa[:, 0:128], onesf, onesf[0:1, :], gnrow, gnrow[0:1, :], True, True)
    P.op("dve", lambda e: e.tensor_copy(out=gno[:], in_=fa[:, 0:128]), reads=[fa], writes=[gno])
    P.dma("sp", xst[0][0:1, :], dr["mix_norm0"].rearrange("(o n) -> o n", o=1), writes=[xst[0]])
    for half in range(2):
        mm(P, fa, fa[:, :], onesf, onesf[0:1, :], xst[0], xst[0][0:1, half * 512:(half + 1) * 512], True, True)
        P.op("dve", lambda e: e.tensor_copy(out=C.gbc[:, half * 512:(half + 1) * 512], in_=fa[:, :]), reads=[fa], writes=[C.gbc])

    def emit_out(src, r0, nr, c0):
        if "ag_chunks" in dr:
            for (ci, off, sw, so) in pad_segments(HALO + c0, 512):
                tl = Tile(None, "ago")
                P.dma("sp", dr["ag_chunks"][ci][r0:r0 + nr, off:off + sw], src[:, so:so + sw], reads=[src], writes=[tl])
                outs.append((ci, tl))
        else:
            tl = Tile(None, "oTo")
            P.dma("sp", dr["oT_out"][r0:r0 + nr, c0:c0 + 512], src[:, :], reads=[src], writes=[tl])
            outs.append(tl)

    xd = dr["x"]

    def nrm(G):
        hg = hnTg[G % 2]
        pend = None
        for jt in range(4):
            i = 4 * G + jt
            xs = xst[i % 2]
            P.dma("sp", xs[:], xd[i * 128:(i + 1) * 128, :], writes=[xs])
            if pend is not None:
                norm_TA_b(P, C, pend[0], hg, pend[1])
            pend = (norm_TA_a(P, C, xs, xs[:]), jt * 128)
            yield
        norm_TA_b(P, C, pend[0], hg, pend[1])
        yield

    def fmtm(G):
        hg = hnTg[G % 2]
        gp = G % 2
        fmb = [F0, F1]
        nfm = [0]

        def proj(col0, m):
            b = fmb[nfm[0] % 2]
            nfm[0] += 1
            for k in range(8):
                mm(P, b, b[0:m, :], w_sb, w_sb[:, k, col0:col0 + m], hg, hg[:, k, :], k == 0, k == 7)
            return b
        b = proj(0, 64)
        P.op("act", lambda e: e.mul(out=gqT[gp][:], in_=b[0:64, :], mul=0.125), reads=[b], writes=[gqT[gp]])
        yield
        b = proj(64, 64)
        P.op("dve", lambda e: e.tensor_copy(out=gkT[gp][:], in_=b[0:64, :]), reads=[b], writes=[gkT[gp]])
        yield
        b = proj(128, 16)
        P.op("dve", lambda e: e.tensor_copy(out=glrT[gp][:], in_=b[0:16, :]), reads=[b], writes=[glrT[gp]])
        yield
        items = [(144, gqc[:, 1:2], gqc, qnT2), (272, gkc[:, 0:1], gkc, knT2)]
        for n_, (col0, gc_ap, gc_t, dst_t) in enumerate(items):
            b = proj(col0, 128)
            sq_, sd_ = sqv[n_ % 2], sdt[n_ % 2]
            P.op("act", lambda e: e.activation(out=sq_[:], in_=b[:, :], func=AF.Square), reads=[b], writes=[sq_])
            yield
            mm(P, F2, F2[:, :], BDb, BDb[:, :], sq_, sq_[:], True, True)
            P.op("act", lambda e: e.activation(out=sd_[:], in_=F2[:, :], func=AF.Ln, scale=1.0 / 64, bias=C.epsc[:, 0:1]),
                 reads=[F2, C.epsc], writes=[sd_])
            P.op("act", lambda e: e.activation(out=sd_[:], in_=sd_[:], func=AF.Exp, scale=-0.5), reads=[sd_], writes=[sd_])
            P.op("dve", lambda e: e.scalar_tensor_tensor(out=dst_t[:, 512 * G:512 * G + 512], in0=b[:, :], scalar=gc_ap, in1=sd_[:],
                                                         op0=ALU.mult, op1=ALU.mult), reads=[b, gc_t, sd_], writes=[dst_t])
            yield
        for jt in range(4):
            i = 4 * G + jt
            cg = jt * 128
            pB = fmb[jt % 2]
            for k in range(8):
                mm(P, pB, pB[:, 0:448], hg, hg[:, k, cg:cg + 128], w_sb, w_sb[:, k, 400:848], k == 0, k == 7)
            P.op("dve", lambda e: e.tensor_copy(out=vg[gp][:, jt, :], in_=pB[:, 0:128]), reads=[pB], writes=[vg[gp]])
            P.op("act", lambda e: e.activation(out=sr[gp][:, jt, :], in_=pB[:, 128:256], func=AF.Silu), reads=[pB], writes=[sr[gp]])
            P.op("act", lambda e: e.copy(out=vsb[:, i, :], in_=pB[:, 256:384]), reads=[pB], writes=[vsb])
            P.op("dve", lambda e: e.tensor_copy(out=gktm[gp][:, jt, :], in_=pB[:, 384:448]), reads=[pB], writes=[gktm[gp]])
            yield

    def gla(G):
        gp = G % 2
        for jt in range(4):
            mm(P, GA, GA[:, jt * 64:(jt + 1) * 64], glrT[gp], glrT[gp][:, jt * 128:(jt + 1) * 128], wg2, wg2[:, :], True, False)
            mm(P, GA, GA[:, jt * 64:(jt + 1) * 64], C.onesb, C.onesb[0:1, :], bgrow, bgrow[0:1, :], False, True)
        yield
        P.op("act", lambda e: e.activation(out=e1[:], in_=GA[:, 0:256], func=AF.Exp, scale=-1.0), reads=[GA], writes=[e1])
        P.op("act", lambda e: e.activation(out=lt[:], in_=e1[:], func=AF.Ln, bias=1.0), reads=[e1], writes=[lt])
        yield
        for jt in range(4):
            mm(P, GB, GB[0:64, jt * 128:(jt + 1) * 128], lt, lt[:, jt, :], Dm, Dm[:], True, True)
        for jt in range(4):
            mm(P, GC, GC[0:64, jt * 128:(jt + 1) * 128], lt, lt[:, jt, :], Bm, Bm[:], True, True)
        for jt in range(4):
            mm(P, GA, GA[:, 256 + jt * 64:256 + (jt + 1) * 64], Dm, Dm[:], lt, lt[:, jt, :], True, True)
        yield
        P.op("act", lambda e: e.activation(out=kdT[:], in_=GB[0:64, :], func=AF.Exp), reads=[GB], writes=[kdT])
        P.op("act", lambda e: e.activation(out=qdT[:], in_=GC[0:64, :], func=AF.Exp), reads=[GC], writes=[qdT])
        P.op("act", lambda e: e.activation(out=kdt[:], in_=GA[:, 256:512], func=AF.Exp), reads=[GA], writes=[kdt])
        yield
        P.op("dve", lambda e: e.tensor_tensor(out=kendT[:], in0=gkT[gp][:], in1=kdT[:], op=ALU.mult), reads=[gkT[gp], kdT], writes=[kendT])
        gq3 = gqT[gp][:].rearrange("p (j t) -> p j t", j=4)
        P.op("dve", lambda e: e.tensor_tensor(out=qe0[:, :, 0:64], in0=gq3[:, :, 0:64], in1=qdT[:, :, 0:64], op=ALU.mult),
             reads=[gqT[gp], qdT], writes=[qe0])
        P.op("dve", lambda e: e.tensor_tensor(out=qe1[:, :, 64:128], in0=gq3[:, :, 64:128], in1=qdT[:, :, 64:128], op=ALU.mult),
             reads=[gqT[gp], qdT], writes=[qe1])
        P.op("dve", lambda e: e.tensor_tensor(out=ke0[0:64, :, :], in0=gktm[gp][0:64, :, :], in1=kdt[0:64, :, :], op=ALU.mult),
             reads=[gktm[gp], kdt], writes=[ke0])
        P.op("dve", lambda e: e.tensor_tensor(out=ke1[64:128, :, :], in0=gktm[gp][64:128, :, :], in1=kdt[64:128, :, :], op=ALU.mult),
             reads=[gktm[gp], kdt], writes=[ke1])
        yield
        for jt in range(4):
            mm(P, GB, GB[:, jt * 128:(jt + 1) * 128], kendT, kendT[:, jt, :], gqT[gp], gqT[gp][:, jt * 128:(jt + 1) * 128], True, True)
        yield
        P.op("dve", lambda e: e.tensor_tensor(out=scT[:], in0=GB[:, :], in1=BD4[:], op=ALU.mult), reads=[GB, BD4], writes=[scT])
        yield
        for jt in range(4):
            mm(P, GC, GC[0:64, jt * 128:(jt + 1) * 128], ke0, ke0[:, jt, :], vg[gp], vg[gp][:, jt, :], True, True)
        yield
        for jt in range(4):
            mm(P, GB, GB[0:64, jt * 128:(jt + 1) * 128], ke1, ke1[:, jt, :], vg[gp], vg[gp][:, jt, :], True, True)
        yield
        for jt in range(4):
            for half, ub in ((0, GC), (1, GB)):
                c = 8 * G + 2 * jt + half
                s_in, s_out = Sf[c % NSR], Sf[(c + 1) % NSR]
                P.op("dve", lambda e: e.scalar_tensor_tensor(out=s_out[:], in0=s_in[:], scalar=qdT[:, jt, 64 * half:64 * half + 1],
                                                             in1=ub[0:64, jt * 128:(jt + 1) * 128], op0=ALU.mult, op1=ALU.add),
                     reads=[s_in, qdT, ub], writes=[s_out])
                sb_o = Sb[(c + 1) % NSR]
                P.op("pool", lambda e: e.tensor_copy(out=sb_o[:], in_=s_out[:]), reads=[s_out], writes=[sb_o])
        yield
        for jt in range(4):
            c = 8 * G + 2 * jt
            oo = GA[:, jt * 128:(jt + 1) * 128]
            mm(P, GA, oo, scT, scT[:, jt, :], vg[gp], vg[gp][:, jt, :], True, False)
            mm(P, GA, oo, qe0, qe0[:, jt, :], Sb[c % NSR], Sb[c % NSR][:], False, False)
            mm(P, GA, oo, qe1, qe1[:, jt, :], Sb[(c + 1) % NSR], Sb[(c + 1) % NSR][:], False, True)
        yield
        for jt in range(4):
            P.op("act", lambda e: e.activation(out=ofin[:, jt, :], in_=GA[:, jt * 128:(jt + 1) * 128], func=AF.Square,
                                               accum_out=ssg[:, jt:jt + 1]), reads=[GA], writes=[ofin, ssg])
        P.op("act", lambda e: e.activation(out=ssg[:, 4:8], in_=ssg[:, 0:4], func=AF.Ln, scale=1.0 / 128, bias=C.epsc[:, 0:1]),
             reads=[ssg, C.epsc], writes=[ssg])
        P.op("act", lambda e: e.activation(out=ssg[:, 8:12], in_=ssg[:, 4:8], func=AF.Exp, scale=-0.5), reads=[ssg], writes=[ssg])
        for jt in range(4):
            P.op("dve", lambda e: e.scalar_tensor_tensor(out=og[:, jt, :], in0=GA[:, jt * 128:(jt + 1) * 128], scalar=ssg[:, 8 + jt:9 + jt],
                                                         in1=gno[:], op0=ALU.mult, op1=ALU.mult), reads=[GA, ssg, gno], writes=[og])
        P.op("dve", lambda e: e.tensor_tensor(out=ofin[:], in0=og[:], in1=sr[gp][:], op=ALU.mult), reads=[og, sr[gp]], writes=[ofin])
        yield
        pt = C.pT[G % 2]
        for jt in range(4):
            P.op("pe", lambda e: e.transpose(out=pt[:, jt, :], in_=ofin[:, jt, :], identity=C.identb[:]), reads=[ofin, C.identb], writes=[pt])
        P.op("act", lambda e: e.copy(out=gstage[gp][:].rearrange("p (j t) -> p j t", j=4), in_=pt[:, 0:4, :]), reads=[pt], writes=[gstage[gp]])
        emit_out(gstage[gp], 0, 128, 512 * G)
        yield

    def step(gen):
        try:
            next(gen)
            return True
        except StopIteration:
            return False

    for G in range(NG + 2):
        gn = nrm(G) if G < NG else iter(())
        gpj = fmtm(G - 1) if 1 <= G <= NG else iter(())
        gg = gla(G - 2) if G >= 2 else iter(())
        alive = [True, True, True]
        rnd = 0
        while any(alive):
            for _ in range(SCHED[0]):
                if alive[1]:
                    alive[1] = step(gpj)
            for _ in range(SCHED[1]):
                if alive[2]:
                    alive[2] = step(gg)
            if alive[0] and (rnd % SCHED[2] == 0 or not (alive[1] or alive[2])):
                alive[0] = step(gn)
            rnd += 1

    P.barrier()
    pst.close()
    pst2 = ExitStack()
    P.pools.append(pst2)
    pz = [P.ps("pz%d" % i, [128, 2, 512], F32, pst2) for i in range(3)]
    pob = [P.ps("pob%d" % i, [128, 512], F32, pst2) for i in range(2)]
    units = []
    for G in range(NG):
        for kt in range(4 * G + 3, -1, -1):
            units.append((G, kt))

    def geom(G, kt):
        j = kt - 4 * G
        diag = j >= 0
        c0 = 128 * j if diag else 0
        cc0 = c0 + 128 if diag else 0
        return diag, c0, cc0

    def stage1(u):
        G, kt = units[u]
        diag, c0, cc0 = geom(G, kt)
        q0 = 512 * G
        z, Et, Lt_ = pz[u % 3], Ef[u % 2], Lb[u % 3]
        for hh in range(2):
            mm(P, z, z[:, hh, c0:512], knT2, knT2[64 * hh:64 * hh + 64, kt * 128:(kt + 1) * 128],
               qnT2, qnT2[64 * hh:64 * hh + 64, q0 + c0:q0 + 512], True, False, True)
        P.op("act", lambda e: e.activation(out=Et[:, :, c0:512], in_=z[:, :, c0:512], func=AF.Exp), reads=[z], writes=[Et])
        P.op("act", lambda e: e.activation(out=Lt_[:, :, c0:512], in_=Et[:, :, c0:512], func=AF.Ln, bias=1.0), reads=[Et], writes=[Lt_])
        if diag:
            for hh in range(2):
                P.op("dve", lambda e: e.tensor_tensor(out=Lt_[:, hh, c0:c0 + 128], in0=Lt_[:, hh, c0:c0 + 128], in1=Ms[:], op=ALU.mult),
                     reads=[Lt_, Ms], writes=[Lt_])
            P.op("dve", lambda e: e.tensor_copy(out=Rf[:, :, c0:c0 + 128], in_=Lt_[:, :, c0:c0 + 128]), reads=[Lt_], writes=[Rf])
        if cc0 < 512:
            P.op("dve", lambda e: e.tensor_tensor(out=Rf[:, :, cc0:512], in0=Rf[:, :, cc0:512], in1=Lt_[:, :, cc0:512], op=ALU.add),
                 reads=[Rf, Lt_], writes=[Rf])
        rb = Rb[(u + 1) % 4]
        P.op("dve", lambda e: e.tensor_copy(out=rb[:, :, c0:512], in_=Rf[:, :, c0:512]), reads=[Rf], writes=[rb])

    def stage2(u):
        G, kt = units[u]
        diag, c0, cc0 = geom(G, kt)
        z, Lt_, Wt, rb = pz[u % 3], Lb[u % 3], Wb[u % 2], Rb[u % 4]
        for hh in range(2):
            mm(P, z, z[:, hh, c0:512], TriN, TriN[:], Lt_, Lt_[:, hh, c0:512], False, cc0 >= 512, True)
            if cc0 < 512:
                mm(P, z, z[:, hh, cc0:512], OnesN, OnesN[:], rb, rb[:, hh, cc0:512], False, True, True)
        P.op("act", lambda e: e.activation(out=Wt[:, :, c0:512], in_=z[:, :, c0:512], func=AF.Exp), reads=[z], writes=[Wt])
        if diag:
            for hh in range(2):
                P.op("dve", lambda e: e.tensor_tensor(out=Wt[:, hh, c0:c0 + 128], in0=Wt[:, hh, c0:c0 + 128], in1=Ms[:], op=ALU.mult),
                     reads=[Wt, Ms], writes=[Wt])

    def stage3(u):
        G, kt = units[u]
        diag, c0, cc0 = geom(G, kt)
        q0 = 512 * G
        Wt = Wb[u % 2]
        first = kt == 4 * G + 3
        last = kt == 0
        for hh in range(2):
            o = pob[hh]
            vv = vsb[:, kt, hh * 64:(hh + 1) * 64]
            mm(P, o, o[0:64, c0:512], vsb, vv, Wt, Wt[:, hh, c0:512], first, last, True)
            if last:
                og_ = ostage[hh]
                P.op("dve", lambda e: e.tensor_copy(out=og_[:], in_=o[0:64, :]), reads=[o], writes=[og_])
                emit_out(og_, 128 + 64 * hh, 64, q0)
                if hh == 1 and "ag_ready" in dr:
                    dr["ag_ready"](q0 + 512, outs)

    nu = len(units)
    if DBG <= 6:
        nu = 0
    for u in range(min(2, nu)):
        stage1(u)
    for u in range(nu):
        stage2(u)
        if u + 2 < nu:
            stage1(u + 2)
        if u >= 1:
            stage3(u - 1)
    if nu:
        stage3(nu - 1)
    return outs


A_INPUTS = [("x", [SEQ, D], F32), ("w_in", [D, NWA], F32), ("w_gate2", [16, 64], F32), ("b_gate", [64], F32),
            ("gla_norm", [128], F32), ("q_norm", [64], F32), ("k_norm", [64], F32), ("mix_norm0", [D], F32),
            ("ident", [128, 128], F32), ("ones", [128, 128], F32), ("epsc", [128, 1], F32),
            ("Dm", [128, 128], F32), ("Bm", [128, 128], F32), ("BD", [128, 128], F32), ("Ms", [128, 128], F32),
            ("TriN", [128, 128], F32), ("OnesN", [128, 128], F32)]


def build_a(ntiles=SEQ // 128):
    nc = bass.Bass("TRN2", target_bir_lowering=False)
    dr = {}
    for n, s, dt in A_INPUTS:
        if n == "x":
            s = [ntiles * 128, D]
        dr[n] = dram_in(nc, n, s, dt)
    dr["oT_out"] = nc.dram_tensor("oT_out", [256, ntiles * 128], BF16, kind="ExternalOutput").ap()
    P = Prog(nc)
    outs = phase_a(P, dr, ntiles)
    P.finish(outs)
    P.close()
    return nc


def a_consts():
    t = np.arange(128)
    same = (t[:, None] // 64) == (t[None, :] // 64)
    c = {"ident": np.eye(128, dtype=np.float32), "ones": np.ones((128, 128), np.float32),
         "epsc": np.full((128, 1), EPS, np.float32),
         "Dm": np.where(same & (t[:, None] > t[None, :]), -1.0 / 16, 0.0).astype(np.float32),
         "Bm": np.where(same, -1.0 / 16, 0.0).astype(np.float32),
         "BD": same.astype(np.float32),
         "Ms": (t[:, None] < t[None, :]).astype(np.float32),
         "TriN": np.where(t[:, None] >= t[None, :], -1.0, 0.0).astype(np.float32),
         "OnesN": np.full((128, 128), -1.0, np.float32)}
    return c


def a_in_maps(inp, ntiles=SEQ // 128):
    w = inp["hy_w_in"][0]
    cuts = np.cumsum([0, 256, 256, 512, 512, 16, 512, 512, 512])
    o_gq, o_gk, o_gv, o_gr, o_glr, o_sq, o_sk, o_sv = cuts[:8]
    consts = a_consts()
    maps = []
    for c in range(NCORES):
        b, g = c // 4, c % 4
        cols = np.concatenate([
            np.arange(o_gq + 64 * g, o_gq + 64 * g + 64), np.arange(o_gk + 64 * g, o_gk + 64 * g + 64),
            np.arange(o_glr, o_glr + 16),
            np.arange(o_sq + 128 * g, o_sq + 128 * g + 128), np.arange(o_sk + 128 * g, o_sk + 128 * g + 128),
            np.arange(o_gv + 128 * g, o_gv + 128 * g + 128), np.arange(o_gr + 128 * g, o_gr + 128 * g + 128),
            np.arange(o_sv + 128 * g, o_sv + 128 * g + 128), np.arange(o_gk + 64 * g, o_gk + 64 * g + 64)])
        assert len(cols) == NWA
        m = {"x": inp["x"][b, :ntiles * 128], "w_in": w[:, cols], "w_gate2": inp["hy_w_gate2"][0][:, 64 * g:64 * g + 64],
             "b_gate": inp["hy_b_gate"][0][64 * g:64 * g + 64], "gla_norm": inp["hy_gla_norm"][0],
             "q_norm": inp["hy_sb_q_norm"][0], "k_norm": inp["hy_sb_k_norm"][0], "mix_norm0": inp["mix_norm"][0]}
        m.update(consts)
        maps.append({k: np.ascontiguousarray(v) for k, v in m.items()})
    return maps


def assemble_oT(res_a, T=SEQ):
    oT_full = np.zeros((2, D, T), res_a[0]["oT_out"].dtype)
    for c in range(NCORES):
        b, g = c // 4, c % 4
        o = res_a[c]["oT_out"]
        oT_full[b, 128 * g:128 * g + 128] = o[0:128]
        oT_full[b, 512 + 128 * g:512 + 128 * g + 128] = o[128:256]
    return oT_full


def build_fused():
    nc = bass.Bass("TRN2", target_bir_lowering=False)
    dr = {}
    for n, sh, dt in A_INPUTS + B_INPUTS + [("sel", [128, 4], F32)]:
        if n in dr or n == "oT":
            continue
        dr[n] = dram_in(nc, n, sh, dt)
    dr["out"] = nc.dram_tensor("out", [TOK, D], F32, kind="ExternalOutput").ap()
    ag_in = [nc.dram_tensor("ag_in%d" % i, [256, AGC], BF16) for i in range(NAGC)]
    ag_out = [nc.dram_tensor("ag_out%d" % i, [4 * 256, AGC], BF16) for i in range(NAGC)]
    P = Prog(nc)
    zst = P.scope()
    zt = P.sb("zt", [128, HALO], BF16, zst)
    P.op("pool", lambda e: e.memset(zt[:], 0.0), writes=[zt])
    ztoks = []
    for r in range(2):
        tl = Tile(None, "ag_zero%d" % r)
        P.dma("sp", ag_in[0].ap()[r * 128:(r + 1) * 128, 0:HALO], zt[:], reads=[zt], writes=[tl])
        ztoks.append((0, tl))
    dra = dict(dr)
    dra["ag_chunks"] = [t.ap() for t in ag_in]
    ag_state = {"next": 0, "tok": None}

    def ag_ready(ntok_done, outs_):
        while ag_state["next"] < NAGC and (ag_state["next"] + 1) * AGC <= HALO + ntok_done:
            ci = ag_state["next"]
            toks = [t.lw for (c_, t) in outs_ + ztoks if c_ == ci]
            ag_state["tok"] = P.collective_allgather(ag_in[ci].ap().opt(), ag_out[ci].ap().opt(), [[0, 1, 2, 3], [4, 5, 6, 7]], toks)
            ag_state["next"] += 1

    dra["ag_ready"] = ag_ready
    outs = phase_a(P, dra)
    ag_ready(SEQ, outs)
    assert ag_state["next"] == NAGC
    ag_tok = ag_state["tok"]
    P.barrier()
    for st_ in reversed(P.pools):
        st_.close()
    P.pools = []
    outd = phase_b(P, dr, ag=([t.ap() for t in ag_out], ag_tok))
    P.finish(outd)
    P.close()
    return nc


def w_out_perm(w_out):
    idx = np.concatenate([np.concatenate([np.arange(128 * g, 128 * g + 128), np.arange(512 + 128 * g, 512 + 128 * g + 128)])
                          for g in range(4)])
    return w_out[idx]


def kernel(**inp):
    inp = {k: np.asarray(v) for k, v in inp.items()}
    nc = build_fused()
    amaps = a_in_maps(inp)
    bmaps = b_in_maps(inp, None)
    maps = []
    for c in range(NCORES):
        m = dict(amaps[c])
        m.update(bmaps[c])
        m["w_out"] = np.ascontiguousarray(w_out_perm(inp["hy_w_out"][0]))
        sel = np.zeros((128, 4), np.float32)
        sel[:, c % 4] = 1.0
        m["sel"] = sel
        maps.append(m)
    res = run_bass_kernel_spmd(nc, maps, core_ids=list(range(NCORES)))
    out = np.concatenate([r["out"] for r in res.results], 0).reshape(2, SEQ, D)
    return out.astype(np.float32)
```
